# Optimizing a Trainium2 kernel written in Bass

```python
import math
import jax, jax.numpy as jnp
from jax import lax
import numpy as np

D_MODEL = 2048
BATCH = 2
SEQ = 16384
DEPTH = 1
DEC_BATCH = 2
DEC_SEQ = 4096
PAST_LEN = 128

HEAD_DIM = 128
MIX_WIDTH = D_MODEL
A_HEADS = 8
A_KV_HEADS = 2
A_RADIUS = 128
B_HEADS = 8
DILATIONS = ((128, 1), (512, 4), (2048, 16))
N_DIL = len(DILATIONS)
B_RADIUS = 64
NUM_BUCKETS = 32
REL_MAX_DISTANCE = 1024
N_BIAS_HEADS = A_HEADS + N_DIL * B_HEADS
D_FF = 4 * D_MODEL
A_Q_COLS = A_HEADS * HEAD_DIM
A_KV_COLS = A_KV_HEADS * HEAD_DIM
B_COLS = N_DIL * B_HEADS * HEAD_DIM
PROJ_COLS = A_Q_COLS + 2 * A_KV_COLS + 3 * B_COLS
EPS = 1e-6
NEG_INF = -1e30

kernel_name = "hybrid_window_dilated_encoder"


def rms_norm(x, g):
    xf = x.astype(jnp.float32)
    y = xf * lax.rsqrt(jnp.mean(xf * xf, axis=-1, keepdims=True) + EPS)
    return (y * g.astype(jnp.float32)).astype(x.dtype)


def rel_bucket(rel):
    half = NUM_BUCKETS // 2
    exact = half // 2
    n = jnp.abs(rel)
    large = exact + (jnp.log(jnp.maximum(n, 1).astype(jnp.float32) / exact)
                     / math.log(REL_MAX_DISTANCE / exact) * (half - exact)).astype(jnp.int32)
    large = jnp.minimum(large, half - 1)
    return jnp.where(rel > 0, half, 0) + jnp.where(n < exact, n, large)


def to_strided(t, r):
    B, S = t.shape[:2]
    t = t.reshape(B, S // r, r, *t.shape[2:])
    return jnp.moveaxis(t, 2, 1).reshape(B * r, S // r, *t.shape[3:])


def from_strided(t, B, r):
    L = t.shape[1]
    t = t.reshape(B, r, L, *t.shape[2:])
    return jnp.moveaxis(t, 1, 2).reshape(B, L * r, *t.shape[3:])


def banded_attention(q, k, v, bias_cols, radius, dilation, sink=None):
    B, S, KV, G, D = q.shape
    r, blk = dilation, radius
    L = S // r
    nb = -(-L // blk)
    Lp = nb * blk
    q = to_strided(q, r)
    k = to_strided(k, r)
    v = to_strided(v, r)
    N = q.shape[0]
    q = jnp.pad(q, ((0, 0), (0, Lp - L), (0, 0), (0, 0), (0, 0)))
    kpad = ((0, 0), (blk, blk + Lp - L), (0, 0), (0, 0))
    k = jnp.pad(k, kpad)
    v = jnp.pad(v, kpad)
    qb = q.reshape(N, nb, blk, KV, G, D)

    def windows(t):
        t = t.reshape(N, nb + 2, blk, KV, D)
        return jnp.concatenate([t[:, :-2], t[:, 1:-1], t[:, 2:]], axis=2)

    kw, vw = windows(k), windows(v)
    delta = jnp.arange(3 * blk)[None, :] - blk - jnp.arange(blk)[:, None]
    band = jnp.abs(delta) <= radius
    kpos = jnp.arange(nb)[:, None] * blk + jnp.arange(3 * blk)[None, :] - blk
    valid = (kpos >= 0) & (kpos < L)
    mask = band[None] & valid[:, None, :]
    bias = bias_cols[rel_bucket(delta * r)]
    bias = bias.reshape(blk, 3 * blk, KV, G).transpose(2, 3, 0, 1).astype(jnp.float32)
    logits = jnp.einsum('nbqkgd,nbskd->nbkgqs', qb.astype(jnp.float32), kw.astype(jnp.float32)) * (D ** -0.5) + bias
    logits = jnp.where(mask[None, :, None, None], logits, NEG_INF)
    m = jnp.max(logits, axis=-1, keepdims=True)
    if sink is not None:
        s = sink.astype(jnp.float32)[:, :, None, None]
        m = jnp.maximum(m, s)
    p = jnp.exp(logits - m)
    denom = jnp.sum(p, axis=-1, keepdims=True)
    if sink is not None:
        denom = denom + jnp.exp(s - m)
    out = jnp.einsum('nbkgqs,nbskd->nbkgqd', p, vw.astype(jnp.float32)) / denom
    lse = (m + jnp.log(denom))[..., 0]
    out = out.transpose(0, 1, 4, 2, 3, 5).reshape(N, Lp, KV, G, D)[:, :L]
    lse = lse.transpose(0, 1, 4, 2, 3).reshape(N, Lp, KV, G)[:, :L]
    return from_strided(out, B, r), from_strided(lse, B, r)


def encoder_layer(x, c, w_ada, b_ada, norm1_g, w_in, q_norm_a, k_norm_a, sink_a,
                  q_norm_b, k_norm_b, rel_bias, w_out, norm2_g, w1, w2):
    B, S, _ = x.shape
    mod = (jax.nn.silu(c) @ w_ada + b_ada).reshape(B, 6, D_MODEL)[:, :, None, :]
    shift1, scale1, gate1, shift2, scale2, gate2 = [mod[:, i] for i in range(6)]

    h = rms_norm(x, norm1_g) * (1 + scale1) + shift1
    proj = h @ w_in
    cuts = np.cumsum([A_Q_COLS, A_KV_COLS, A_KV_COLS, B_COLS, B_COLS])[:].tolist()
    qa, ka, va, qb, kb, vb = jnp.split(proj, cuts, axis=-1)

    qa = rms_norm(qa.reshape(B, S, A_HEADS, HEAD_DIM), q_norm_a)
    qa = qa.reshape(B, S, A_KV_HEADS, A_HEADS // A_KV_HEADS, HEAD_DIM)
    ka = rms_norm(ka.reshape(B, S, A_KV_HEADS, HEAD_DIM), k_norm_a)
    va = va.reshape(B, S, A_KV_HEADS, HEAD_DIM)
    ya, _ = banded_attention(qa, ka, va, rel_bias[:, :A_HEADS], A_RADIUS, 1,
                             sink=sink_a.reshape(A_KV_HEADS, A_HEADS // A_KV_HEADS))
    ya = ya.reshape(B, S, A_HEADS * HEAD_DIM)

    qb = qb.reshape(B, S, N_DIL, B_HEADS, HEAD_DIM)
    kb = kb.reshape(B, S, N_DIL, B_HEADS, HEAD_DIM)
    vb = vb.reshape(B, S, N_DIL, B_HEADS, HEAD_DIM)
    outs, lses = [], []
    for g, (_, r) in enumerate(DILATIONS):
        qg = rms_norm(qb[:, :, g], q_norm_b[g])[:, :, :, None, :]
        kg = rms_norm(kb[:, :, g], k_norm_b[g])
        cols = rel_bias[:, A_HEADS + g * B_HEADS: A_HEADS + (g + 1) * B_HEADS]
        o, lse = banded_attention(qg, kg, vb[:, :, g], cols, B_RADIUS, r)
        outs.append(o)
        lses.append(lse)
    alpha = jax.nn.softmax(jnp.stack(lses, axis=0), axis=0)
    yb = jnp.sum(alpha[..., None] * jnp.stack(outs, axis=0), axis=0).reshape(B, S, B_HEADS * HEAD_DIM)

    y = jnp.concatenate([ya, yb], axis=-1).astype(x.dtype) @ w_out
    x = x + gate1 * y

    h2 = rms_norm(x, norm2_g) * (1 + scale2) + shift2
    f = jnp.square(jax.nn.relu(h2 @ w1)) @ w2
    return x + gate2 * f


def trunk(x, c, w_ada, b_ada, norm1_g, w_in, q_norm_a, k_norm_a, sink_a,
          q_norm_b, k_norm_b, rel_bias, w_out, norm2_g, w1, w2):
    for l in range(DEPTH):
        x = encoder_layer(x, c, w_ada[l], b_ada[l], norm1_g[l], w_in[l], q_norm_a[l], k_norm_a[l],
                          sink_a[l], q_norm_b[l], k_norm_b[l], rel_bias, w_out[l], norm2_g[l],
                          w1[l], w2[l])
    return x


def setup_inputs(seed: int = 0) -> dict:
    key = jax.random.key(seed)
    ks = jax.random.split(key, 20)
    f32 = jnp.float32
    nrm = lambda k, shape, s: jax.random.normal(k, shape, f32) * s
    return {
        "x_prompt": nrm(ks[0], (BATCH, SEQ, D_MODEL), 1.0),
        "x_sample": nrm(ks[1], (DEC_BATCH, DEC_SEQ, D_MODEL), 1.0),
        "c_prompt": nrm(ks[2], (BATCH, D_MODEL), 1.0),
        "c_sample": nrm(ks[3], (DEC_BATCH, D_MODEL), 1.0),
        "w_ada": nrm(ks[4], (DEPTH, D_MODEL, 6 * D_MODEL), 0.5 * D_MODEL ** -0.5),
        "b_ada": nrm(ks[5], (DEPTH, 6 * D_MODEL), 0.02),
        "norm1_g": 1.0 + nrm(ks[6], (DEPTH, D_MODEL), 0.02),
        "w_in": nrm(ks[7], (DEPTH, D_MODEL, PROJ_COLS), D_MODEL ** -0.5),
        "q_norm_a": 1.0 + nrm(ks[8], (DEPTH, HEAD_DIM), 0.02),
        "k_norm_a": 1.0 + nrm(ks[9], (DEPTH, HEAD_DIM), 0.02),
        "sink_a": nrm(ks[10], (DEPTH, A_HEADS), 1.0),
        "q_norm_b": 1.0 + nrm(ks[11], (DEPTH, N_DIL, HEAD_DIM), 0.02),
        "k_norm_b": 1.0 + nrm(ks[12], (DEPTH, N_DIL, HEAD_DIM), 0.02),
        "rel_bias": nrm(ks[13], (NUM_BUCKETS, N_BIAS_HEADS), 0.5),
        "w_out": nrm(ks[14], (DEPTH, MIX_WIDTH, D_MODEL), MIX_WIDTH ** -0.5),
        "norm2_g": 1.0 + nrm(ks[15], (DEPTH, D_MODEL), 0.02),
        "w1": nrm(ks[16], (DEPTH, D_MODEL, D_FF), D_MODEL ** -0.5),
        "w2": nrm(ks[17], (DEPTH, D_FF, D_MODEL), D_FF ** -0.5),
    }


def reference(x_prompt, x_sample, c_prompt, c_sample, w_ada, b_ada, norm1_g, w_in, q_norm_a,
              k_norm_a, sink_a, q_norm_b, k_norm_b, rel_bias, w_out, norm2_g, w1, w2):
    y_prompt = trunk(x_prompt, c_prompt, w_ada, b_ada, norm1_g, w_in, q_norm_a, k_norm_a, sink_a,
                     q_norm_b, k_norm_b, rel_bias, w_out, norm2_g, w1, w2)
    y_sample = trunk(x_sample, c_sample, w_ada, b_ada, norm1_g, w_in, q_norm_a, k_norm_a, sink_a,
                     q_norm_b, k_norm_b, rel_bias, w_out, norm2_g, w1, w2)
    return (y_prompt, y_sample)
```

```python
import math
import numpy as np
import concourse.bass as bass
import concourse.mybir as mybir
from concourse.bass_utils import run_bass_kernel_spmd

F32, BF16 = mybir.dt.float32, mybir.dt.bfloat16
ALU = mybir.AluOpType
AF = mybir.ActivationFunctionType

D = 2048
NKC = 16
DFF = 8192
PROJ = 10752
T = 512
HALO = 1024
LP, LS = 4096, 1024
EP, ES = LP + 2 * HALO, LS + 2 * HALO
NEXT = EP + ES
NTOK = LP + LS
NKV = 26
VW = 130
J = 640
JH = 320
EPS = 1e-6
NEG = -30000.0
ENG = ("sp", "act", "dve", "pool", "pe")
CFG = ((1, 128), (1, 64), (4, 64), (16, 64))


class Prog:
    def __init__(self, nc):
        self.nc = nc
        self.ops = {k: [] for k in ENG}
        self.esem = {k: nc.alloc_semaphore("es_" + k) for k in ENG}
        self.ecount = {k: 0 for k in ENG}
        self.waited = {k: {} for k in ENG}
        self.dsem = {}
        self.dcount = {}
        self.lastw = {}
        self.readers = {}

    def _waits(self, eng, reads, writes, extra):
        need = {}

        def add(ev):
            sem, val, src = ev
            if src == eng and eng == "pe":
                return
            if need.get(sem, 0) < val:
                need[sem] = val

        for b in reads:
            if b in self.lastw:
                add(self.lastw[b])
        for b in writes:
            if b in self.lastw:
                add(self.lastw[b])
            for ev in self.readers.get(b, {}).values():
                add(ev)
        for ev in extra:
            add(ev)
        out = []
        w = self.waited[eng]
        for sem, val in need.items():
            if w.get(sem, 0) < val:
                w[sem] = val
                out.append((sem, val))
        return out

    def _record(self, ev, reads, writes):
        for b in writes:
            self.lastw[b] = ev
            self.readers[b] = {}
        for b in reads:
            r = self.readers.setdefault(b, {})
            old = r.get(ev[0])
            if old is None or old[1] < ev[1]:
                r[ev[0]] = ev

    def op(self, eng, fn, reads=(), writes=(), extra=()):
        waits = self._waits(eng, reads, writes, extra)
        self.ecount[eng] += 1
        n = self.ecount[eng]
        sem = self.esem[eng]

        def run(e, fn=fn, waits=waits, sem=sem):
            for s, v in waits:
                e.wait_ge(s, v)
            ins = fn(e)
            ins.then_inc(sem, 1)

        self.ops[eng].append(run)
        ev = (sem, n, eng)
        self._record(ev, reads, writes)
        return ev

    def dma(self, q, out, in_, reads=(), writes=(), skey=None, extra=(), **kw):
        waits = self._waits(q, reads, writes, extra)
        if skey not in self.dsem:
            self.dsem[skey] = self.nc.alloc_semaphore("ds_%d" % len(self.dsem))
            self.dcount[skey] = 0
        self.dcount[skey] += 16
        v = self.dcount[skey]
        sem = self.dsem[skey]

        def run(e, waits=waits, sem=sem, out=out, in_=in_, kw=kw):
            for s, vv in waits:
                e.wait_ge(s, vv)
            e.dma_start(out=out, in_=in_, **kw).then_inc(sem, 16)

        self.ops[q].append(run)
        ev = (sem, v, "dma")
        self._record(ev, reads, writes)
        return ev

    def final_wait(self, eng, evs):
        waits = self._waits(eng, (), (), evs)

        def run(e, waits=waits):
            for s, v in waits:
                e.wait_ge(s, v)

        self.ops[eng].append(run)


def sap(ap_base, offset_elems, dims):
    return bass.AP(ap_base.tensor, ap_base.offset + offset_elems, [list(ap_base.ap[0])] + [list(d) for d in dims])


def build(dbg=None, stages=("p0", "p1", "p2")):
    nc = bass.Bass("TRN2", target_bir_lowering=False)
    P = Prog(nc)

    def din(name, shape, dt=F32):
        return nc.dram_tensor(name, list(shape), dt, kind="ExternalInput").ap()

    xe = din("xe", [NEXT, D])
    c2 = din("c2", [2, D])
    valid_in = din("valid", [128, NEXT // 128])
    ind_in = din("ind", [4, 33, J])
    ident_in = din("ident", [128, 128])
    w_in = din("w_in", [D, PROJ])
    w_out = din("w_out", [D, D])
    w1 = din("w1", [D, DFF])
    w2 = din("w2", [DFF, D])
    w_ada = din("w_ada", [D, 6 * D])
    b_ada = din("b_ada", [1, 6 * D])
    n1g_in = din("norm1_g", [1, D])
    n2g_in = din("norm2_g", [1, D])
    qna = din("q_norm_a", [1, 128])
    kna = din("k_norm_a", [1, 128])
    sink_in = din("sink_a", [1, 8])
    qnb = din("q_norm_b", [3, 128])
    knb = din("k_norm_b", [3, 128])
    relb = din("rel_bias", [32, 32])
    yo = nc.dram_tensor("yo", [NTOK, D], F32, kind="ExternalOutput").ap()

    def dscr(name, shape, dt):
        return nc.dram_tensor(name, list(shape), dt, kind=("ExternalOutput" if dbg else "Internal")).ap()

    Win_s = dscr("Win_s", [D, PROJ], BF16)
    Wout_s = dscr("Wout_s", [D, D], BF16)
    W1_s = dscr("W1_s", [D, DFF], BF16)
    W2_s = dscr("W2_s", [DFF, D], BF16)
    mod_s = dscr("mod_s", [2, 6 * D], F32)
    tv_s = dscr("tv_s", [32, J], F32)
    KT_s = dscr("KT_s", [NKV, 128, NEXT], BF16)
    V_s = dscr("V_s", [NEXT, NKV, VW], BF16)

    if dbg:
        dbgQ = nc.dram_tensor("dbgQ", [128, 32, T], BF16, kind="ExternalOutput").ap()
        dbgY = nc.dram_tensor("dbgY", [128, NKC, T], BF16, kind="ExternalOutput").ap()

    def sb(name, shape, dt):
        return nc.alloc_sbuf_tensor("sb_" + name, list(shape), dt)

    xres = sb("xres", [128, 4, D], F32)
    hT = sb("hT", [128, NKC, T], BF16)
    QT = sb("QT", [128, 32, T], BF16)
    YT = sb("YT", [128, NKC, T], BF16)
    KW = [sb("KW0", [128, 768], BF16), sb("KW1", [128, 1024], BF16), sb("KW2", [128, 2560], BF16)]
    VT0 = sb("VT0", [128, 6, VW], BF16)
    VT1 = sb("VT1", [128, 4, 2, VW], BF16)
    VT2 = sb("VT2", [128, 16, 2, VW], BF16)
    Wr = [sb("Wr%d" % i, [128, NKC, 512], BF16) for i in range(2)]
    biasA = sb("biasA", [128, 2, 3, 4, 128], BF16)
    biasB = sb("biasB", [128, 3, 8, 2, 128], BF16)
    xs = sb("xs", [128, D], BF16)
    tmpf = [sb("tmpf%d" % i, [128, 512], F32) for i in range(2)]
    denf = sb("denf", [128, 512], F32)
    tmpb = [sb("tmpb%d" % i, [128, 512], BF16) for i in range(4)]
    sqb = [sb("sqb%d" % i, [128, 512], BF16) for i in range(2)]
    rsf = [sb("rsf%d" % i, [128, 512], F32) for i in range(2)]
    kst = [sb("kst%d" % i, [128, 512], BF16) for i in range(2)]
    vst = [sb("vst%d" % i, [128, 4, VW], BF16) for i in range(2)]
    gbc = sb("gbc", [128, 2, D], BF16)
    modP = sb("modP", [128, 2, 6, NKC], F32)
    n12g = sb("n12g", [128, 2, NKC], F32)
    scsh = sb("scsh", [128, 2, 4, NKC], F32)
    gains = sb("gains", [128, 8], F32)
    validt = sb("validt", [128, NEXT // 128], F32)
    ones_col = sb("ones_col", [128, 4, 2], F32)
    ident = sb("ident", [128, 128], BF16)
    ones_bf = sb("ones_bf", [128, 128], BF16)
    stat = sb("stat", [128, 4], F32)
    epsc = sb("epsc", [128, 2], F32)
    esrow = sb("esrow", [1, 8, 128], BF16)
    es8 = sb("es8", [1, 16], F32)
    cT = sb("cT", [128, NKC, 2], F32)
    cTb = sb("cTb", [128, NKC, 2], BF16)
    modrow = tmpf
    bada = rsf

    def ps(name, shape, dt=F32):
        return nc.alloc_psum_tensor(name, list(shape), dt)

    psA = [ps("psA%d" % i, [128, 512]) for i in range(2)]
    psS = ps("psS", [128, 512])
    psN = ps("psN", [128, 512])
    psTf = [ps("psT%d" % i, [128, 512]) for i in range(2)]
    psT = [t.bitcast(BF16) for t in psTf]
    psL = [ps("psL%d" % i, [128, 512]) for i in range(2)] + psTf
    PSLK = [("psL", 0), ("psL", 1), ("psT", 0), ("psT", 1)]

    ctr = {"w": 0, "a": 0, "i": 0, "v": 0, "k": 0, "l": 0}

    def nxt(k, n):
        v = ctr[k] % n
        ctr[k] += 1
        return v

    SQ128 = math.sqrt(128.0)
    RS = NKV * VW

    cast_ev = {}

    def do_cast1(src, dst, rows, name, i, npc=8):
        step = rows // npc
        cast_ev[name] = P.dma("pool", dst[i * step:(i + 1) * step, :], src[i * step:(i + 1) * step, :], skey=("cast", name))

    def do_cast_cols(i, npc, c0, c1, name):
        step = D // npc
        cast_ev[name] = P.dma("pool", Win_s[i * step:(i + 1) * step, c0:c1], w_in[i * step:(i + 1) * step, c0:c1], skey=("cast", name))

    later_q = [(i, 16, 0, 1024, "Win_q") for i in range(16)] + [(i, 16, 1536, 4608, "Win_q") for i in range(16)]
    later_casts = [(w_out, Wout_s, D, "Wout_s", i, 16) for i in range(16)] + [(w1, W1_s, D, "W1_s", i, 64) for i in range(64)] + \
                  [(w2, W2_s, DFF, "W2_s", i, 64) for i in range(64)]

    def p0():
        for i in range(8):
            do_cast_cols(i, 8, 4608, PROJ, "Win_kv")
        for i in range(2):
            do_cast_cols(i, 2, 1024, 1536, "Win_kv")
        P.dma("pool", ident[:, :], ident_in[:, :], writes=["ident"], skey="c0")
        P.dma("sp", validt[:, :], valid_in[:, :], writes=["validt"], skey="c1")
        P.op("dve", lambda e: e.memset(ones_bf[:, :], 1.0), writes=["ones_bf"])
        P.op("dve", lambda e: e.memset(ones_col[:, :, :], 1.0), writes=["ones_col"])
        for col, src in ((0, qna[0:1, :]), (1, kna[0:1, :]), (2, qnb[0:1, :]), (3, qnb[1:2, :]), (4, qnb[2:3, :]),
                         (5, knb[0:1, :]), (6, knb[1:2, :]), (7, knb[2:3, :])):
            P.dma("sp", gains[:, col:col + 1], bass.AP(src.tensor, src.offset, [[1, 128], [1, 1]]),
                  writes=["gains"], skey="c1")
        P.op("dve", lambda e: e.tensor_scalar(gains[:, 0:1], gains[:, 0:1], 1.0 / SQ128, None, ALU.mult), reads=["gains"], writes=["gains"])
        P.op("dve", lambda e: e.tensor_scalar(gains[:, 2:5], gains[:, 2:5], 1.0 / SQ128, None, ALU.mult), reads=["gains"], writes=["gains"])
        P.op("dve", lambda e: e.memset(epsc[:, :], EPS), writes=["epsc"])
        for i, src in enumerate((n1g_in, n2g_in)):
            P.dma("sp", n12g[:, i, :], bass.AP(src.tensor, src.offset, [[1, 128], [128, NKC]]),
                  writes=["n12g"], skey="c1", allow_slow_non_contiguous=True)
        P.dma("sp", es8[0:1, 0:8], sink_in[0:1, :], writes=["es8"], skey="c1")
        P.op("act", lambda e: e.activation(es8[0:1, 8:16], es8[0:1, 0:8], AF.Exp), reads=["es8"], writes=["es8b"])
        esb = es8[0:1, 8:16]
        P.op("dve", lambda e: e.tensor_copy(esrow[0:1, :, :], bass.AP(esb.tensor, esb.offset, [list(esb.ap[0]), [1, 8], [0, 128]])),
             reads=["es8b"], writes=["esrow"])
        xf = xres[:, :, :].rearrange("p a b -> p (a b)")
        Rext = xres[0:33, 3, 1024:1056]
        indt = xres[0:33, 2, 0:J]
        tvrow = xres[0:8, 3, 0:J]
        P.dma("sp", xres[0:32, 3, 1024:1056], relb[:, :], writes=[("x", 3)], skey="c3")
        P.op("dve", lambda e: e.memset(xres[32:33, 3, 1024:1056], 1.0), writes=[("x", 3)], reads=[("x", 3)])
        def bias_cfg(cfg):
            P.dma("sp", indt, ind_in[cfg], writes=[("x", 2)], skey="c3")

            def mmb(e, cfg=cfg):
                e.matmul(psA[0][0:8, 0:320], Rext[:, 8 * cfg:8 * cfg + 8], indt[:, 0:320], start=True, stop=True)
                return e.matmul(psA[1][0:8, 0:320], Rext[:, 8 * cfg:8 * cfg + 8], indt[:, 320:640], start=True, stop=True)
            P.op("pe", mmb, reads=[("x", 3), ("x", 2)], writes=[("psA", 0), ("psA", 1)])
            P.op("dve", lambda e: e.tensor_copy(tvrow[:, 0:320], psA[0][0:8, 0:320]), reads=[("psA", 0)], writes=[("x", 3)])
            P.op("dve", lambda e: e.tensor_copy(tvrow[:, 320:640], psA[1][0:8, 0:320]), reads=[("psA", 1)], writes=[("x", 3)])
            P.dma("sp", tv_s[8 * cfg:8 * cfg + 8, :], tvrow, reads=[("x", 3)], writes=["tv_s"], skey="c4")
            cmin = -128 if cfg == 0 else -64
            wdt = 384 if cfg == 0 else 256
            base = JH + cmin - 127
            for hh in range(8):
                P.dma("sp", xf[:, hh * 384:hh * 384 + wdt],
                      bass.AP(tv_s.tensor, tv_s.offset + (8 * cfg + hh) * J + base, [[1, 128], [1, wdt]]),
                      reads=["tv_s"], writes=[("x", 0), ("x", 1)], skey="c5")
            for hh in range(8):
                nt = 3 if cfg == 0 else 2
                for t_ in range(nt):
                    hk = xf[:, hh * 384 + t_ * 128 + 127:hh * 384 + t_ * 128 + 128]
                    rev = bass.AP(hk.tensor, hk.offset, [list(hk.ap[0]), [-1, 128]])
                    if cfg == 0:
                        dst = biasA[:, hh // 4, t_, hh % 4, :]
                    else:
                        dst = biasB[:, cfg - 1, hh, t_, :]
                    P.op("act", lambda e, dst=dst, rev=rev: e.activation(dst, rev, AF.Copy), reads=[("x", 0), ("x", 1)], writes=["bias"])
        for s_ in range(2):
            P.dma("sp", cT[:, :, s_], bass.AP(c2.tensor, c2.offset + s_ * D, [[1, 128], [128, NKC]]),
                  writes=["cT"], skey="c1", allow_slow_non_contiguous=True)
        P.op("act", lambda e: e.activation(cTb[:, :, :], cT[:, :, :], AF.Silu), reads=["cT"], writes=["cTb"])
        for ch in range(24):
            if ch in (3, 9, 15, 21):
                bias_cfg((ch - 3) // 6)
            c0 = ch * 512
            s = nxt("w", 2)
            P.dma("pool", Wr[s][:, :, :], w_ada[:, c0:c0 + 512].rearrange("(kc p) c -> p kc c", p=128),
                  writes=[("W", s)], skey=("W", s))
            m = ch % 2
            bsrc = b_ada[0:1, c0:c0 + 512]
            P.dma("sp", bada[m][0:2, :], bass.AP(bsrc.tensor, bsrc.offset, [[0, 2], [1, 512]]),
                  writes=[("rsf", m)], skey=("bada", m))
            a = nxt("a", 2)

            def mm(e, s=s, a=a):
                for kc in range(NKC):
                    ins = e.matmul(psA[a][0:2, :], cTb[:, kc, :], Wr[s][:, kc, :], start=(kc == 0), stop=(kc == NKC - 1))
                return ins
            P.op("pe", mm, reads=[("W", s), "cTb"], writes=[("psA", a)])
            P.op("dve", lambda e, a=a, m=m: e.tensor_tensor(modrow[m][0:2, :], psA[a][0:2, :], bada[m][0:2, :], ALU.add),
                 reads=[("psA", a), ("rsf", m)], writes=[("tmpf", m)])
            P.dma("sp", mod_s[:, c0:c0 + 512], modrow[m][0:2, :], reads=[("tmpf", m)], writes=["mod_s"], skey=("mods", m))
        for seg in range(2):
            for i in range(6):
                P.dma("sp", modP[:, seg, i, :],
                      bass.AP(mod_s.tensor, mod_s.offset + seg * 6 * D + i * D, [[1, 128], [128, NKC]]),
                      reads=["mod_s"], writes=["modP"], skey="c2", allow_slow_non_contiguous=True)
        for seg in range(2):
            P.op("dve", lambda e, seg=seg: e.scalar_tensor_tensor(scsh[:, seg, 0, :], modP[:, seg, 1, :], 1.0, n12g[:, 0, :], ALU.add, ALU.mult),
                 reads=["modP", "n12g"], writes=["scsh"])
            P.op("dve", lambda e, seg=seg: e.tensor_copy(scsh[:, seg, 1, :], modP[:, seg, 0, :]), reads=["modP"], writes=["scsh"])
            P.op("dve", lambda e, seg=seg: e.scalar_tensor_tensor(scsh[:, seg, 2, :], modP[:, seg, 4, :], 1.0, n12g[:, 1, :], ALU.add, ALU.mult),
                 reads=["modP", "n12g"], writes=["scsh"])
            P.op("dve", lambda e, seg=seg: e.tensor_copy(scsh[:, seg, 3, :], modP[:, seg, 3, :]), reads=["modP"], writes=["scsh"])
    def wload(scr, r0, c0, width, name):
        s = nxt("w", 2)
        P.dma("sp", Wr[s][:, :, 0:width], scr[r0:r0 + D, c0:c0 + width].rearrange("(kc p) c -> p kc c", p=128),
              writes=[("W", s)], skey=("W", s), extra=[cast_ev[name]])
        return s

    HB = [dict(t=hT, i=0), dict(t=QT[:, 0:16, :], i=1)]

    def hkey(hb, tt):
        return ("hT", hb["i"], tt)

    def norm_a(seg, which, tt, hb, src=None, skey=None):
        if src is None:
            src, skey = xres[:, tt, :], ("x", tt)
        skeys = skey if isinstance(skey, list) else [skey]
        h = hb["t"]
        P.op("dve", lambda e: e.memset(stat[:, 0:1], 0.0), writes=["stat"])
        P.op("act", lambda e: e.activation(h[:, :, tt * 128:(tt + 1) * 128], src.rearrange("p (a b) -> p a b", a=NKC), AF.Square, accum_out=stat[:, 0:1]),
             reads=skeys + ["stat"], writes=[hkey(hb, tt), "stat"])
        P.op("act", lambda e: e.activation(stat[:, 1:2], stat[:, 0:1], AF.Ln, bias=epsc[:, 0:1], scale=1.0 / D), reads=["stat", "epsc"], writes=["stat"])
        P.op("act", lambda e: e.activation(stat[:, 2:3], stat[:, 1:2], AF.Exp, scale=-0.5), reads=["stat"], writes=["stat"])
        P.op("dve", lambda e: e.tensor_scalar(xs[:, :], src, stat[:, 2:3], None, ALU.mult),
             reads=skeys + ["stat"], writes=["xs"])

    def norm_b(seg, which, tt, hb):
        h = hb["t"]
        for half in range(2):
            def tr(e, half=half):
                for j in range(8):
                    kc = half * 8 + j
                    ins = e.transpose(psT[half][:, j * 128:(j + 1) * 128], xs[:, kc * 128:(kc + 1) * 128], ident[:, :])
                return ins
            P.op("pe", tr, reads=["xs", "ident"], writes=[("psT", half)])

            def ev(js, on_act, half=half):
                def f(e):
                    for j in js:
                        kc = half * 8 + j
                        o = h[:, kc, tt * 128:(tt + 1) * 128]
                        i_ = psT[half][:, j * 128:(j + 1) * 128]
                        sc = scsh[:, seg, 2 * which, kc:kc + 1]
                        sh = scsh[:, seg, 2 * which + 1, kc:kc + 1]
                        if on_act:
                            ins = e.activation(o, i_, AF.Identity, bias=sh, scale=sc)
                        else:
                            ins = e.tensor_scalar(o, i_, sc, sh, ALU.mult, ALU.add)
                    return ins
                return f
            P.op("act", ev([0, 1, 2], True), reads=[("psT", half), "scsh"], writes=[hkey(hb, tt)])
            P.op("dve", ev([3, 4, 5, 6, 7], False), reads=[("psT", half), "scsh"], writes=[hkey(hb, tt)])

    def norm_tile(seg, which, tt, hb=None):
        hb = hb or HB[0]
        norm_a(seg, which, tt, hb)
        norm_b(seg, which, tt, hb)

    def norm_steps(seg, which, hb, srcs=None, pre=None):
        def mk(t_a, t_b):
            def f():
                if t_b is not None:
                    norm_b(seg, which, t_b, hb)
                if t_a is not None:
                    if pre:
                        pre(t_a)
                    if srcs:
                        norm_a(seg, which, t_a, hb, *srcs(t_a))
                    else:
                        norm_a(seg, which, t_a, hb)
            return f
        return [mk(0, None), mk(1, 0), mk(2, 1), mk(3, 2), mk(None, 3)]

    def ht_all(hb):
        return [hkey(hb, t_) for t_ in range(4)]
    HT_ALL = ht_all(HB[0])
    pend = []

    def flush_pend():
        while pend:
            pend.pop(0)()

    def qk_chunk(s, j, gcol, dest, dkey, after=None, hb=None):
        hb = hb or HB[0]
        hTt = hb["t"]
        a = nxt("a", 2)
        b = nxt("i", 2)

        def mmr(k0, k1):
            def mm(e):
                for kc in range(k0, k1):
                    ins = e.matmul(psA[a][:, :], Wr[s][:, kc, j * 128:(j + 1) * 128], hTt[:, kc, :], start=(kc == 0), stop=(kc == NKC - 1))
                return ins
            return mm
        P.op("pe", mmr(0, 5), reads=[("W", s)] + ht_all(hb), writes=[("psA", a)])
        flush_pend()
        P.op("pe", mmr(5, NKC), reads=[("W", s)] + ht_all(hb), writes=[("psA", a)])

        def tail():
            P.op("act", lambda e: e.activation(sqb[b][:, :], psA[a][:, :], AF.Square), reads=[("psA", a)], writes=[("sqb", b)])
            P.op("pe", lambda e: e.matmul(psS[:, :], ones_bf[:, :], sqb[b][:, :], start=True, stop=True),
                 reads=[("sqb", b), "ones_bf"], writes=["psS"])
            P.op("act", lambda e: e.activation(rsf[b][:, :], psS[:, :], AF.Ln, bias=epsc[:, 0:1], scale=1.0 / 128.0),
                 reads=["psS", "epsc"], writes=[("rsf", b)])
            P.op("act", lambda e: e.activation(rsf[b][:, :], rsf[b][:, :], AF.Exp, scale=-0.5), reads=[("rsf", b)], writes=[("rsf", b)])
            P.op("dve", lambda e: e.scalar_tensor_tensor(dest, psA[a][:, :], gains[:, gcol:gcol + 1], rsf[b][:, :], ALU.mult, ALU.mult),
                 reads=[("psA", a), ("rsf", b), "gains"], writes=[dkey])
            if after:
                after()
        pend.append(tail)

    store_evs = {}

    def p1_xload(seg, eb):
        segbase = 0 if seg == 0 else EP
        E0 = segbase + T * eb
        for tt in range(4):
            P.dma("sp", xres[:, tt, :], xe[E0 + tt * 128:E0 + (tt + 1) * 128, :], writes=[("x", tt)], skey=("x", tt))

    def p1_block(seg, eb, nb, hb, nxtblk, hbn):
        segbase = 0 if seg == 0 else EP
        E0 = segbase + T * eb
        full = 1 <= eb <= nb - 2
        hTt = hb["t"]
        kch = []
        vch = []
        if full:
            kch.append((1024, 256, [0, 1], 1))
            vch.append((1280, 256, 0, 2))
        for g in range(3):
            if not full and g != 2:
                continue
            for j in range(2):
                kch.append((4608 + g * 1024 + j * 512, 512, [2 + g * 8 + j * 4 + q for q in range(4)], 5 + g))
                vch.append((7680 + g * 1024 + j * 512, 512, 2 + g * 8 + j * 4, 4))
        for (c0, width, heads, gcol) in kch:
            s = wload(Win_s, 0, c0, width, "Win_kv")
            if later_q:
                do_cast_cols(*later_q.pop(0))
            elif later_casts:
                do_cast1(*later_casts.pop(0))
            for j, kvh in enumerate(heads):
                k = nxt("k", 2)

                def after(k=k, kvh=kvh):
                    store_evs[("k", k)] = P.dma("pool", KT_s[kvh, :, E0:E0 + T], kst[k][:, :], reads=[("kst", k)], skey=("kst", k))
                qk_chunk(s, j, gcol, kst[k][:, :], ("kst", k), after, hb=hb)
        flush_pend()
        steps = []
        xl = []
        if nxtblk is not None:
            nE0 = (0 if nxtblk[0] == 0 else EP) + T * nxtblk[1]
            xl = [lambda tt=tt: P.dma("sp", xres[:, tt, :], xe[nE0 + tt * 128:nE0 + (tt + 1) * 128, :], writes=[("x", tt)], skey=("x", tt))
                  for tt in range(4)]
            xl_all = list(xl)

            def need_x(t_):
                while len(xl) > 3 - t_:
                    xl.pop(0)()
            steps = norm_steps(nxtblk[0], 0, hbn, pre=need_x)
        ng = len(vch) * 4
        pos = {(i + 1) * ng // 6: i for i in range(5)} if steps else {}
        gi = 0
        for (c0, width, hv0, nh) in vch:
            s = wload(Win_s, 0, c0, width, "Win_kv")
            if later_q:
                do_cast_cols(*later_q.pop(0))
            elif later_casts:
                do_cast1(*later_casts.pop(0))
            for _ in range(2):
                if xl:
                    xl.pop(0)()
            for tt in range(4):
                a = nxt("a", 2)
                v = nxt("v", 2)
                tile = (E0 + tt * 128) // 128

                def mm(e, s=s, a=a, tt=tt, width=width):
                    for kc in range(NKC):
                        ins = e.matmul(psA[a][:, 0:width], hTt[:, kc, tt * 128:(tt + 1) * 128], Wr[s][:, kc, 0:width],
                                       start=(kc == 0), stop=(kc == NKC - 1))
                    return ins
                P.op("pe", mm, reads=[("W", s), hkey(hb, tt)], writes=[("psA", a)])
                P.op("act", lambda e, a=a, v=v, nh=nh, width=width, tile=tile: e.activation(
                    vst[v][:, 0:nh, 0:128], psA[a][:, 0:width].rearrange("p (h d) -> p h d", h=nh), AF.Copy,
                    scale=validt[:, tile:tile + 1]), reads=[("psA", a), "validt"], writes=[("vst", v)])
                P.op("dve", lambda e, v=v, nh=nh, tile=tile: e.tensor_scalar(
                    vst[v][:, 0:nh, 128:130], ones_col[:, 0:nh, :], validt[:, tile:tile + 1], None, ALU.mult),
                    reads=["ones_col", "validt", ("vst", v)], writes=[("vst", v)])
                store_evs[("v", v)] = P.dma("pool", V_s[E0 + tt * 128:E0 + (tt + 1) * 128, hv0:hv0 + nh, :], vst[v][:, 0:nh, :],
                                            reads=[("vst", v)], skey=("vst", v))
                gi += 1
                if gi in pos:
                    steps[pos[gi]]()
                    pos.pop(gi)
        for gi_ in sorted(pos):
            steps[pos[gi_]]()

    afifo = []
    ADEPTH = 4

    def a_push(fn, tags):
        afifo.append([fn, True, tags])
        while sum(1 for x in afifo if x[1]) >= ADEPTH:
            a_pop()

    def a_after(fn):
        if not afifo:
            fn()
        else:
            afifo.append([fn, False, ()])

    def a_pop():
        fn, counted, tags = afifo.pop(0)
        fn()
        while afifo and not afifo[0][1]:
            afifo.pop(0)[0]()

    def a_flush(tag=None):
        while afifo and (tag is None or any(tag in x[2] for x in afifo)):
            a_pop()

    def attn_units(units, num, den, nkey, dkey, depth=3):
        def qk(u, l):
            P.op("pe", lambda e: e.matmul(u.get("lgm", u["lg"])(psL[l]), u["lhsT"], u["rhs"], start=True, stop=True),
                 reads=u["rk"], writes=[PSLK[l]])
            P.op("dve", lambda e: e.tensor_tensor(u["lg"](psL[l]), u["lg"](psL[l]), u["bias"], ALU.add),
                 reads=[PSLK[l], "bias"], writes=[PSLK[l]])
            P.op("act", lambda e: e.activation(u["tf"](tmpb[l]), u["lg"](psL[l]), AF.Exp),
                 reads=[PSLK[l]], writes=[("tmpb", l)])

        def pv(u, l):
            def f(e):
                e.matmul(u["on"](num), u["v"], u["tf"](tmpb[l]), start=u["start"], stop=u["stop"], skip_group_check=True)
                return e.matmul(u["on"](den), u["vone"], u["tf"](tmpb[l]), start=u["start"], stop=u["stop"], skip_group_check=True)
            P.op("pe", f, reads=[("tmpb", l)] + u["vk"], writes=[nkey, dkey])
        for u in units:
            l = nxt("l", 4)
            qk(u, l)
            a_push(lambda u=u, l=l: pv(u, l), tuple(u["vk"]) + tuple(k for k in u["rk"] if isinstance(k, str)))

    def attn_super(units, num, den, nkey, dkey, depth=3):
        def qk(u, l):
            nk, nqp, npart = u["nk"], u["nqp"], len(u["parts"])
            tot = nqp * npart

            def f(e):
                for p_, pt in enumerate(u["parts"]):
                    ins = e.matmul(psL[l][0:nk, p_ * nqp:(p_ + 1) * nqp], pt["lhsT"], pt["rhs"], start=True, stop=True, skip_group_check=True)
                return ins
            P.op("pe", f, reads=u["rk"], writes=[PSLK[l]])
            P.op("dve", lambda e: e.tensor_tensor(psL[l][0:nk, 0:tot].rearrange("p (a b) -> p a b", a=npart),
                                                  psL[l][0:nk, 0:tot].rearrange("p (a b) -> p a b", a=npart), u["bias3"], ALU.add),
                 reads=[PSLK[l], "bias"], writes=[PSLK[l]])
            P.op("act", lambda e: e.activation(tmpb[l][0:nk, 0:tot], psL[l][0:nk, 0:tot], AF.Exp),
                 reads=[PSLK[l]], writes=[("tmpb", l)])

        def pv(u, l):
            nk, nqp = u["nk"], u["nqp"]

            def f(e):
                for p_, pt in enumerate(u["parts"]):
                    st = bool(u["first"] and p_ == 0)
                    e.matmul(pt["on"](num), pt["v"], tmpb[l][0:nk, p_ * nqp:(p_ + 1) * nqp], start=st, stop=False, skip_group_check=True)
                    ins = e.matmul(pt["on"](den), pt["vone"], tmpb[l][0:nk, p_ * nqp:(p_ + 1) * nqp], start=st, stop=False, skip_group_check=True)
                return ins
            P.op("pe", f, reads=[("tmpb", l)] + u["vk"], writes=[nkey, dkey])
        for u in units:
            l = nxt("l", 4)
            qk(u, l)
            a_push(lambda u=u, l=l: pv(u, l), tuple(u["vk"]) + tuple(k for k in u["rk"] if isinstance(k, str)))

    def bc128(ap_col, nk):
        return bass.AP(ap_col.tensor, ap_col.offset, [[ap_col.ap[0][0], nk], [0, 128]])

    def strided(t, col0, step, n, npart=128):
        base = t[0:npart, col0:col0 + 1]
        return bass.AP(base.tensor, base.offset, [[base.ap[0][0], npart], [step, n]])

    ND = [(psN, psS, "psN", "psS"), (psA[0], psA[1], ("psA", 0), ("psA", 1))]
    ndc = [0]

    YTf = YT.bitcast(F32)[:, :, :].rearrange("p a b -> p (a b)")
    YT_KEYS = [("YT", i) for i in range(10)]
    ystores = []

    def p2_block(seg, b, prenormed, nxtblk):
        segbase = 0 if seg == 0 else EP
        E0 = segbase + HALO + T * b
        orow = (0 if seg == 0 else LP) + T * b
        if not prenormed:
            for tt in range(4):
                P.dma("sp", xres[:, tt, :], xe[E0 + tt * 128:E0 + (tt + 1) * 128, :], writes=[("x", tt)], skey=("x", tt))
            for tt in range(4):
                norm_tile(seg, 0, tt)
        qch = [(0, list(range(0, 4)), 0), (512, list(range(4, 8)), 0)]
        for g in range(3):
            for j in range(2):
                qch.append((1536 + g * 1024 + j * 512, [8 + g * 8 + j * 4 + q for q in range(4)], 2 + g))
        for qi, (c0, heads, gcol) in enumerate(qch):
            s = wload(Win_s, 0, c0, 512, "Win_q")
            if qi == 1:
                while ystores:
                    ystores.pop(0)()
            for j, hq in enumerate(heads):
                qk_chunk(s, j, gcol, QT[:, hq, :], ("QT", hq))
        flush_pend()
        if prenormed:
            for tt in range(4):
                P.dma("sp", xres[:, tt, :], xe[E0 + tt * 128:E0 + (tt + 1) * 128, :], writes=[("x", tt)], skey=("x", tt))
        kv_extra = [v for k, v in store_evs.items() if k[0] in ('k', 'v')]
        for kh in range(2):
            a_flush("VT0")
            a_flush("KW0")
            P.dma("sp", KW[0][:, 0:768], KT_s[kh, :, E0 - 128:E0 + 640], writes=["KW0"], skey="KW0", extra=kv_extra)
            P.dma("sp", VT0[:, 0:6, :], V_s[E0 - 128:E0 + 640, kh, :].rearrange("(j p) c -> p j c", p=128),
                  writes=["VT0"], skey="VT0", extra=kv_extra)
            for qb in range(4):
                num, den, nkey, dkey = ND[ndc[0] % 2]
                ndc[0] += 1
                units = []
                for rel in range(3):
                    j = qb + rel
                    units.append(dict(
                        lgm=lambda p_: p_[:, :].rearrange("p (h q) -> p h q", h=4),
                        lg=lambda p_: p_[:, :],
                        tf=lambda t_: t_[:, :],
                        lhsT=KW[0][:, j * 128:(j + 1) * 128],
                        rhs=QT[:, 4 * kh:4 * kh + 4, qb * 128:(qb + 1) * 128],
                        rk=["KW0"] + [("QT", 4 * kh + q) for q in range(4)],
                        bias=biasA[:, kh, rel, :, :].rearrange("p h q -> p (h q)"),
                        v=VT0[:, j, 0:128], vone=bc128(VT0[:, j, 128:129], 128), vk=["VT0"],
                        on=lambda n_: n_[:, :], start=(rel == 0), stop=False))
                attn_units_A(units, num, den, nkey, dkey)

                def fin_a(num=num, den=den, nkey=nkey, dkey=dkey, kh=kh, qb=qb):
                    P.op("pe", lambda e: e.matmul(den[:, :], ones_bf[0:1, 0:128],
                                                  esrow[0:1, 4 * kh:4 * kh + 4, :].rearrange("p h q -> p (h q)"),
                                                  start=False, stop=True, skip_group_check=True),
                         reads=["esrow", "ones_bf"], writes=[dkey])
                    P.op("act", lambda e: e.activation(denf[:, :], den[:, :], AF.Ln), reads=[dkey], writes=["denf"])
                    P.op("act", lambda e: e.activation(denf[:, :], denf[:, :], AF.Exp, scale=-1.0), reads=["denf"], writes=["denf"])
                    P.op("dve", lambda e: e.tensor_tensor(
                        YT[:, 4 * kh:4 * kh + 4, qb * 128:(qb + 1) * 128], num[:, :].rearrange("p (h q) -> p h q", h=4),
                        denf[:, :].rearrange("p (h q) -> p h q", h=4), ALU.mult),
                        reads=[nkey, "denf"], writes=[("YT", kh)])
                a_after(fin_a)
        for h in range(8):
            num, den, nkey, dkey = ND[ndc[0] % 2]
            ndc[0] += 1
            for g in range(3):
                r = CFG[g + 1][0]
                hv = 2 + 8 * g + h
                hq = 8 + 8 * g + h
                wd = T + 128 * r
                kwk = "KW%d" % g
                vtk = "VT%d" % g
                if g == 2:
                    for q4 in range(4):
                        a_flush(("VT2", q4))
                else:
                    a_flush(vtk)
                a_flush(kwk)
                P.dma("sp", KW[g][:, 0:wd], KT_s[hv, :, E0 - 64 * r:E0 - 64 * r + wd], writes=[kwk], skey=kwk, extra=kv_extra)
                vbase = V_s[E0 - 64 * r:E0 - 64 * r + 1, hv, :]
                if g == 0:
                    P.dma("sp", VT0[:, 0:5, :], V_s[E0 - 64:E0 + 576, hv, :].rearrange("(j p) c -> p j c", p=128),
                          writes=[vtk], skey=vtk, extra=kv_extra)
                elif g == 1:
                    P.dma("sp", VT1[:, :, :, :],
                          bass.AP(vbase.tensor, vbase.offset, [[4 * RS, 128], [RS, 4], [512 * RS, 2], [1, VW]]),
                          writes=[vtk], skey=vtk, extra=kv_extra)
                else:
                    for q4 in range(4):
                        P.dma("sp", VT2[:, 4 * q4:4 * q4 + 4, 0, :],
                              bass.AP(vbase.tensor, vbase.offset + 4 * q4 * RS, [[16 * RS, 128], [RS, 4], [1, VW]]),
                              writes=[("VT2", q4)], skey=("VT2", q4), extra=kv_extra)
                        P.dma("sp", VT2[0:32, 4 * q4:4 * q4 + 4, 1, :],
                              bass.AP(vbase.tensor, vbase.offset + (16 * 128 + 4 * q4) * RS, [[16 * RS, 32], [RS, 4], [1, VW]]),
                              writes=[("VT2", q4)], skey=("VT2", q4), extra=kv_extra)
                units = []
                nqp = 128 if g < 2 else 32
                npart = 4
                combos = [(0, kt) for kt in range(2)] if g < 2 else [(q4, kt) for q4 in range(4) for kt in range(2)]
                for (q4_, kt) in combos:
                    nk = 32 if (g == 2 and kt == 1) else 128
                    parts = []
                    vkk = [("VT2", q4_)] if g == 2 else [vtk]
                    for c_ in range(4 * q4_, 4 * q4_ + 4):
                        if g == 0:
                            j = c_ + kt
                            lhsT = KW[0][:, j * 128:(j + 1) * 128]
                            vt = VT0[:, j, :]
                            qcol = lambda t_, c_=c_: t_[:, c_ * 128:(c_ + 1) * 128]
                            rhs = QT[:, hq, c_ * 128:(c_ + 1) * 128]
                        else:
                            lhsT = strided(KW[g], c_ + r * 128 * kt, r, nk)
                            vt = (VT1 if g == 1 else VT2)[0:nk, c_, kt, :]
                            qcol = lambda t_, c_=c_, r=r, nqp=nqp: strided(t_, c_, r, nqp)
                            rhs = strided_q(QT, hq, c_, r, nqp)
                        parts.append(dict(lhsT=lhsT, rhs=rhs, v=vt[:, 0:128], vone=bc128(vt[:, 128:129], nk), on=qcol))
                    bsl = biasB[0:nk, g, h, kt, 0:nqp]
                    units.append(dict(parts=parts, nk=nk, nqp=nqp, rk=[kwk, ("QT", hq)], vk=vkk,
                                      bias3=bass.AP(bsl.tensor, bsl.offset, [list(bsl.ap[0]), [0, npart], [1, nqp]]),
                                      first=(g == 0 and kt == 0 and q4_ == 0)))
                attn_super(units, num, den, nkey, dkey)
            def fin_b(num=num, den=den, nkey=nkey, dkey=dkey, h=h):
                P.op("act", lambda e: e.activation(denf[:, :], den[:, :], AF.Ln), reads=[dkey], writes=["denf"])
                P.op("act", lambda e: e.activation(denf[:, :], denf[:, :], AF.Exp, scale=-1.0), reads=["denf"], writes=["denf"])
                P.op("dve", lambda e: e.tensor_tensor(YT[:, 8 + h, :], num[:, :], denf[:, :], ALU.mult),
                     reads=[nkey, "denf"], writes=[("YT", 2 + h)])
            a_after(fin_b)
        a_flush()
        YT_ALL = [("YT", i) for i in range(10)]
        if dbg and seg == 0 and b == 0:
            P.dma("sp", dbgQ[:, :, :], QT[:, :, :], reads=[("QT", i) for i in range(32)], skey="dbg")
            P.dma("sp", dbgY[:, :, :], YT[:, :, :], reads=YT_ALL, skey="dbg")
        for pair in range(2):
            for cc in range(4):
                s = wload(Wout_s, 0, cc * 512, 512, "Wout_s")
                for tt in (2 * pair, 2 * pair + 1):
                    a = nxt("a", 2)
                    i_ = nxt("i", 2)

                    def mm(e, s=s, a=a, tt=tt):
                        for kc in range(NKC):
                            ins = e.matmul(psA[a][:, :], YT[:, kc, tt * 128:(tt + 1) * 128], Wr[s][:, kc, :], start=(kc == 0), stop=(kc == NKC - 1))
                        return ins
                    P.op("pe", mm, reads=[("W", s)] + YT_ALL, writes=[("psA", a)])
                    P.op("dve", lambda e, a=a, i_=i_, cc=cc: e.tensor_tensor(tmpf[i_][:, :], psA[a][:, :], gbc[:, 0, cc * 512:(cc + 1) * 512], ALU.mult),
                         reads=[("psA", a), "gbc"], writes=[("tmpf", i_)])
                    P.op("pool", lambda e, i_=i_, tt=tt, cc=cc: e.tensor_tensor(xres[:, tt, cc * 512:(cc + 1) * 512], tmpf[i_][:, :],
                                                                              xres[:, tt, cc * 512:(cc + 1) * 512], ALU.add),
                         reads=[("tmpf", i_), ("x", tt)], writes=[("x", tt)])
                if pair == 1:
                    if cc == 0:
                        norm_a(seg, 1, 0, HB[0])
                    elif cc == 1:
                        norm_b(seg, 1, 0, HB[0])
                        norm_a(seg, 1, 1, HB[0])
                    elif cc == 2:
                        norm_b(seg, 1, 1, HB[0])
        for tt in (2, 3):
            norm_a(seg, 1, tt, HB[0])
            norm_b(seg, 1, tt, HB[0])
        FT_ALL = [("QT", i) for i in range(32)]
        accs = [(psA[0], ("psA", 0)), (psA[1], ("psA", 1)), (psS, "psS"), (psN, "psN")]
        for half in range(2):
            for c8 in range(8):
                s = wload(W1_s, 0, half * 4096 + c8 * 512, 512, "W1_s")
                for j in range(4):
                    ffl = c8 * 4 + j
                    a = nxt("a", 2)
                    i_ = nxt("i", 2)

                    def mm(e, s=s, a=a, j=j):
                        for kc in range(NKC):
                            ins = e.matmul(psA[a][:, :], Wr[s][:, kc, j * 128:(j + 1) * 128], hT[:, kc, :], start=(kc == 0), stop=(kc == NKC - 1))
                        return ins
                    P.op("pe", mm, reads=[("W", s)] + HT_ALL, writes=[("psA", a)])
                    P.op("act", lambda e, a=a, i_=i_: e.activation(tmpf[i_][:, :], psA[a][:, :], AF.Relu),
                         reads=[("psA", a)], writes=[("tmpf", i_)])
                    P.op("pool", lambda e, i_=i_, ffl=ffl: e.tensor_tensor(QT[:, ffl, :], tmpf[i_][:, :], tmpf[i_][:, :], ALU.mult),
                         reads=[("tmpf", i_)], writes=[("QT", ffl)])
            pre = (half == 1 and nxtblk is not None)
            if pre:
                nseg, nb_ = nxtblk
                En = (0 if nseg == 0 else EP) + HALO + T * nb_

                def stage(t_):
                    sl = t_ % 2
                    P.dma("sp", YTf[:, sl * D:(sl + 1) * D], xe[En + t_ * 128:En + (t_ + 1) * 128, :],
                          writes=YT_KEYS + [("YTs", sl)], skey=("YTs", sl))

                def na(t_):
                    sl = t_ % 2
                    norm_a(nseg, 0, t_, HB[0], YTf[:, sl * D:(sl + 1) * D], [("YTs", sl)] + YT_KEYS)
                stage(0)
                stage(1)
                na(0)
            for cc in range(4):
                for q in range(2):
                    if pre and cc == 3 and q == 1:
                        norm_b(nseg, 0, 3, HB[0])
                    s = wload(W2_s, half * 4096 + q * D, cc * 512, 512, "W2_s")
                    for tt in range(4):
                        acc, akey = accs[tt]

                        def mm(e, s=s, acc=acc, tt=tt, q=q):
                            for kc in range(NKC):
                                ins = e.matmul(acc[:, :], QT[:, q * 16 + kc, tt * 128:(tt + 1) * 128], Wr[s][:, kc, :],
                                               start=(q == 0 and kc == 0), stop=(q == 1 and kc == NKC - 1), skip_group_check=True)
                            return ins
                        P.op("pe", mm, reads=[("W", s)] + FT_ALL, writes=[akey])
                for tt in range(4):
                    acc, akey = accs[tt]
                    i_ = nxt("i", 2)
                    P.op("dve", lambda e, acc=acc, i_=i_, cc=cc: e.tensor_tensor(tmpf[i_][:, :], acc[:, :], gbc[:, 1, cc * 512:(cc + 1) * 512], ALU.mult),
                         reads=[akey, "gbc"], writes=[("tmpf", i_)])
                    P.op("pool", lambda e, i_=i_, tt=tt, cc=cc: e.tensor_tensor(xres[:, tt, cc * 512:(cc + 1) * 512], tmpf[i_][:, :],
                                                                              xres[:, tt, cc * 512:(cc + 1) * 512], ALU.add),
                         reads=[("tmpf", i_), ("x", tt)], writes=[("x", tt)])
                if pre:
                    if cc == 0:
                        norm_b(nseg, 0, 0, HB[0])
                        na(1)
                        stage(2)
                    elif cc == 1:
                        norm_b(nseg, 0, 1, HB[0])
                        na(2)
                        stage(3)
                    elif cc == 2:
                        norm_b(nseg, 0, 2, HB[0])
                        na(3)
        def do_store(orow=orow):
            for tt in range(4):
                store_evs[("y", tt)] = P.dma("sp", yo[orow + tt * 128:orow + (tt + 1) * 128, :], xres[:, tt, :], reads=[("x", tt)], skey=("y", tt))
        if nxtblk is None:
            do_store()
        else:
            ystores.append(do_store)

    def strided_q(t, hq, col0, step, n):
        base = t[:, hq, col0:col0 + 1]
        return bass.AP(base.tensor, base.offset, [list(base.ap[0]), [step, n]])

    def attn_units_A(units, num, den, nkey, dkey):
        attn_units(units, num, den, nkey, dkey)

    def load_gbc(seg):
        for i, blk in enumerate((2, 5)):
            srcp = mod_s[seg:seg + 1, blk * D:(blk + 1) * D]
            P.dma("pool", gbc[:, i, :], bass.AP(srcp.tensor, srcp.offset, [[0, 128], [1, D]]), reads=["mod_s"], writes=["gbc"], skey="gbc")

    p0()
    NBP, NBS = EP // T, ES // T
    if "p1" in stages:
        BL = [(0, eb, NBP) for eb in range(NBP)] + [(1, eb, NBS) for eb in range(NBS)]
        p1_xload(BL[0][0], BL[0][1])
        for tt in range(4):
            norm_tile(BL[0][0], 0, tt, HB[0])
        for i, (sg, eb, nb_) in enumerate(BL):
            p1_block(sg, eb, nb_, HB[i % 2], BL[i + 1] if i + 1 < len(BL) else None, HB[(i + 1) % 2])
    while later_q:
        do_cast_cols(*later_q.pop(0))
    while later_casts:
        do_cast1(*later_casts.pop(0))
    if "p2" in stages:
        B2 = [(0, b) for b in range(LP // T)] + [(1, b) for b in range(LS // T)]
        for i, (sg, b) in enumerate(B2):
            if b == 0:
                load_gbc(sg)
            p2_block(sg, b, i > 0, B2[i + 1] if i + 1 < len(B2) else None)
    P.final_wait("sp", list(store_evs.values()) + [(P.esem[k], P.ecount[k], k) for k in ENG if P.ecount[k] > 0])

    with nc.Block() as block:
        @block.sync
        def _(e):
            for f in P.ops["sp"]:
                f(e)

        @block.scalar
        def _(e):
            for f in P.ops["act"]:
                f(e)

        @block.vector
        def _(e):
            for f in P.ops["dve"]:
                f(e)

        @block.gpsimd
        def _(e):
            for f in P.ops["pool"]:
                f(e)

        @block.tensor
        def _(e):
            for f in P.ops["pe"]:
                f(e)
    return nc


def dbg_emit(L):
    pass


def rel_bucket_np(rel):
    half, exact = 16, 8
    n = np.abs(rel)
    large = exact + (np.log(np.maximum(n, 1).astype(np.float32) / np.float32(exact)).astype(np.float32)
                     / np.float32(math.log(1024 / exact)) * np.float32(half - exact)).astype(np.int32)
    large = np.minimum(large, half - 1)
    return np.where(rel > 0, half, 0) + np.where(n < exact, n, large)


def make_ind():
    ind = np.zeros((4, 33, J), np.float32)
    u = np.arange(J)
    delta = u - JH
    for cfg, (r, rad) in enumerate(CFG):
        inb = np.abs(delta) <= rad
        bk = rel_bucket_np(delta * r)
        ind[cfg, bk[inb], u[inb]] = 1.0
        ind[cfg, 32, ~inb] = NEG
    return ind


_NC_CACHE = {}


def core_inputs(inputs, c):
    s, qt = c // 4, c % 4
    xe = np.zeros((NEXT, D), np.float32)
    valid = np.zeros((NEXT,), np.float32)
    for (arr, L, S, base) in ((inputs["x_prompt"][s], LP, 16384, 0), (inputs["x_sample"][s], LS, 4096, EP)):
        lo = qt * L - HALO
        hi = (qt + 1) * L + HALO
        a, b = max(lo, 0), min(hi, S)
        xe[base + (a - lo):base + (b - lo)] = arr[a:b]
        valid[base + (a - lo):base + (b - lo)] = 1.0
    c2 = np.stack([inputs["c_prompt"][s], inputs["c_sample"][s]]).astype(np.float32)
    return xe, np.ascontiguousarray(valid.reshape(NEXT // 128, 128).T), c2


def kernel(**inputs):
    inputs = {k: np.asarray(v) for k, v in inputs.items()}
    if "nc" not in _NC_CACHE:
        _NC_CACHE["nc"] = build()
    nc = _NC_CACHE["nc"]
    shared = {
        "ind": make_ind(), "ident": np.eye(128, dtype=np.float32),
        "w_in": inputs["w_in"][0], "w_out": inputs["w_out"][0], "w1": inputs["w1"][0], "w2": inputs["w2"][0],
        "w_ada": inputs["w_ada"][0], "b_ada": inputs["b_ada"], "norm1_g": inputs["norm1_g"], "norm2_g": inputs["norm2_g"],
        "q_norm_a": inputs["q_norm_a"], "k_norm_a": inputs["k_norm_a"], "sink_a": inputs["sink_a"],
        "q_norm_b": inputs["q_norm_b"][0], "k_norm_b": inputs["k_norm_b"][0], "rel_bias": inputs["rel_bias"],
    }
    shared = {k: np.ascontiguousarray(v, dtype=np.float32) for k, v in shared.items()}
    in_maps = []
    for c in range(8):
        xe, valid, c2 = core_inputs(inputs, c)
        m = dict(shared)
        m.update({"xe": xe, "valid": valid, "c2": c2})
        in_maps.append(m)
    res = run_bass_kernel_spmd(nc, in_maps, core_ids=list(range(8)))
    yp = np.zeros((2, 16384, D), np.float32)
    ysm = np.zeros((2, 4096, D), np.float32)
    for c in range(8):
        s, qt = c // 4, c % 4
        y = res.results[c]["yo"]
        yp[s, qt * LP:(qt + 1) * LP] = y[0:LP]
        ysm[s, qt * LS:(qt + 1) * LS] = y[LP:LP + LS]
    return (yp, ysm)
```

```python
import math
import numpy as np
import concourse.bass as bass
import concourse.mybir as mybir
from concourse.bass_utils import run_bass_kernel_spmd

F32, BF16 = mybir.dt.float32, mybir.dt.bfloat16
ALU = mybir.AluOpType
AF = mybir.ActivationFunctionType

D = 2048
NKC = 16
DFF = 8192
PROJ = 10752
T = 512
HALO = 1024
LP, LS = 4096, 1024
EP, ES = LP + 2 * HALO, LS + 2 * HALO
NEXT = EP + ES
NTOK = LP + LS
NKV = 26
VW = 130
J = 640
JH = 320
EPS = 1e-6
NEG = -30000.0
ENG = ("sp", "act", "dve", "pool", "pe")
CFG = ((1, 128), (1, 64), (4, 64), (16, 64))


class Prog:
    def __init__(self, nc):
        self.nc = nc
        self.ops = {k: [] for k in ENG}
        self.esem = {k: nc.alloc_semaphore("es_" + k) for k in ENG}
        self.ecount = {k: 0 for k in ENG}
        self.waited = {k: {} for k in ENG}
        self.dsem = {}
        self.dcount = {}
        self.lastw = {}
        self.readers = {}

    def _waits(self, eng, reads, writes, extra):
        need = {}

        def add(ev):
            sem, val, src = ev
            if src == eng and eng == "pe":
                return
            if need.get(sem, 0) < val:
                need[sem] = val

        for b in reads:
            if b in self.lastw:
                add(self.lastw[b])
        for b in writes:
            if b in self.lastw:
                add(self.lastw[b])
            for ev in self.readers.get(b, {}).values():
                add(ev)
        for ev in extra:
            add(ev)
        out = []
        w = self.waited[eng]
        for sem, val in need.items():
            if w.get(sem, 0) < val:
                w[sem] = val
                out.append((sem, val))
        return out

    def _record(self, ev, reads, writes):
        for b in writes:
            self.lastw[b] = ev
            self.readers[b] = {}
        for b in reads:
            r = self.readers.setdefault(b, {})
            old = r.get(ev[0])
            if old is None or old[1] < ev[1]:
                r[ev[0]] = ev

    def op(self, eng, fn, reads=(), writes=(), extra=()):
        waits = self._waits(eng, reads, writes, extra)
        self.ecount[eng] += 1
        n = self.ecount[eng]
        sem = self.esem[eng]

        def run(e, fn=fn, waits=waits, sem=sem):
            for s, v in waits:
                e.wait_ge(s, v)
            ins = fn(e)
            ins.then_inc(sem, 1)

        self.ops[eng].append(run)
        ev = (sem, n, eng)
        self._record(ev, reads, writes)
        return ev

    def dma(self, q, out, in_, reads=(), writes=(), skey=None, extra=(), **kw):
        waits = self._waits(q, reads, writes, extra)
        if skey not in self.dsem:
            self.dsem[skey] = self.nc.alloc_semaphore("ds_%d" % len(self.dsem))
            self.dcount[skey] = 0
        self.dcount[skey] += 16
        v = self.dcount[skey]
        sem = self.dsem[skey]

        def run(e, waits=waits, sem=sem, out=out, in_=in_, kw=kw):
            for s, vv in waits:
                e.wait_ge(s, vv)
            e.dma_start(out=out, in_=in_, **kw).then_inc(sem, 16)

        self.ops[q].append(run)
        ev = (sem, v, "dma")
        self._record(ev, reads, writes)
        return ev

    def final_wait(self, eng, evs):
        waits = self._waits(eng, (), (), evs)

        def run(e, waits=waits):
            for s, v in waits:
                e.wait_ge(s, v)

        self.ops[eng].append(run)


def sap(ap_base, offset_elems, dims):
    return bass.AP(ap_base.tensor, ap_base.offset + offset_elems, [list(ap_base.ap[0])] + [list(d) for d in dims])


def build(dbg=None, stages=("p0", "p1", "p2")):
    nc = bass.Bass("TRN2", target_bir_lowering=False)
    P = Prog(nc)

    def din(name, shape, dt=F32):
        return nc.dram_tensor(name, list(shape), dt, kind="ExternalInput").ap()

    xe = din("xe", [NEXT, D])
    c2 = din("c2", [2, D])
    valid_in = din("valid", [128, NEXT // 128])
    ind_in = din("ind", [4, 33, J])
    ident_in = din("ident", [128, 128])
    w_in = din("w_in", [D, PROJ])
    w_out = din("w_out", [D, D])
    w1 = din("w1", [D, DFF])
    w2 = din("w2", [DFF, D])
    w_ada = din("w_ada", [D, 6 * D])
    b_ada = din("b_ada", [1, 6 * D])
    n1g_in = din("norm1_g", [1, D])
    n2g_in = din("norm2_g", [1, D])
    qna = din("q_norm_a", [1, 128])
    kna = din("k_norm_a", [1, 128])
    sink_in = din("sink_a", [1, 8])
    qnb = din("q_norm_b", [3, 128])
    knb = din("k_norm_b", [3, 128])
    relb = din("rel_bias", [32, 32])
    yo = nc.dram_tensor("yo", [NTOK, D], F32, kind="ExternalOutput").ap()

    def dscr(name, shape, dt):
        return nc.dram_tensor(name, list(shape), dt, kind=("ExternalOutput" if dbg else "Internal")).ap()

    Win_s = dscr("Win_s", [D, PROJ], BF16)
    Wout_s = dscr("Wout_s", [D, D], BF16)
    W1_s = dscr("W1_s", [D, DFF], BF16)
    W2_s = dscr("W2_s", [DFF, D], BF16)
    mod_s = dscr("mod_s", [2, 6 * D], F32)
    tv_s = dscr("tv_s", [32, J], F32)
    KT_s = dscr("KT_s", [NKV, 128, NEXT], BF16)
    V_s = dscr("V_s", [NEXT, NKV, VW], BF16)

    if dbg:
        dbgQ = nc.dram_tensor("dbgQ", [128, 32, T], BF16, kind="ExternalOutput").ap()
        dbgY = nc.dram_tensor("dbgY", [128, NKC, T], BF16, kind="ExternalOutput").ap()

    def sb(name, shape, dt):
        return nc.alloc_sbuf_tensor("sb_" + name, list(shape), dt)

    xres = sb("xres", [128, 4, D], F32)
    hT = sb("hT", [128, NKC, T], BF16)
    QT = sb("QT", [128, 32, T], BF16)
    YT = sb("YT", [128, NKC, T], BF16)
    KW = [sb("KW0", [128, 768], BF16), sb("KW1", [128, 1024], BF16), sb("KW2", [128, 2560], BF16)]
    VT0 = sb("VT0", [128, 6, VW], BF16)
    VT1 = sb("VT1", [128, 4, 2, VW], BF16)
    VT2 = sb("VT2", [128, 16, 2, VW], BF16)
    Wr = [sb("Wr%d" % i, [128, NKC, 512], BF16) for i in range(2)]
    biasA = sb("biasA", [128, 2, 3, 4, 128], BF16)
    biasB = sb("biasB", [128, 3, 8, 2, 128], BF16)
    xs = sb("xs", [128, D], BF16)
    tmpf = [sb("tmpf%d" % i, [128, 512], F32) for i in range(2)]
    denf = sb("denf", [128, 512], F32)
    tmpb = [sb("tmpb%d" % i, [128, 512], BF16) for i in range(4)]
    sqb = [sb("sqb%d" % i, [128, 512], BF16) for i in range(2)]
    rsf = [sb("rsf%d" % i, [128, 512], F32) for i in range(2)]
    kst = [sb("kst%d" % i, [128, 512], BF16) for i in range(2)]
    vst = [sb("vst%d" % i, [128, 4, VW], BF16) for i in range(2)]
    gbc = sb("gbc", [128, 2, D], BF16)
    modP = sb("modP", [128, 2, 6, NKC], F32)
    n12g = sb("n12g", [128, 2, NKC], F32)
    scsh = sb("scsh", [128, 2, 4, NKC], F32)
    gains = sb("gains", [128, 8], F32)
    validt = sb("validt", [128, NEXT // 128], F32)
    ones_col = sb("ones_col", [128, 4, 2], F32)
    ident = sb("ident", [128, 128], BF16)
    ones_bf = sb("ones_bf", [128, 128], BF16)
    stat = sb("stat", [128, 4], F32)
    epsc = sb("epsc", [128, 2], F32)
    esrow = sb("esrow", [1, 8, 128], BF16)
    es8 = sb("es8", [1, 16], F32)
    cT = sb("cT", [128, NKC, 2], F32)
    cTb = sb("cTb", [128, NKC, 2], BF16)
    modrow = tmpf
    bada = rsf

    def ps(name, shape, dt=F32):
        return nc.alloc_psum_tensor(name, list(shape), dt)

    psA = [ps("psA%d" % i, [128, 512]) for i in range(2)]
    psS = ps("psS", [128, 512])
    psN = ps("psN", [128, 512])
    psTf = [ps("psT%d" % i, [128, 512]) for i in range(2)]
    psT = [t.bitcast(BF16) for t in psTf]
    psL = [ps("psL%d" % i, [128, 512]) for i in range(2)] + psTf
    PSLK = [("psL", 0), ("psL", 1), ("psT", 0), ("psT", 1)]

    ctr = {"w": 0, "a": 0, "i": 0, "v": 0, "k": 0, "l": 0}

    def nxt(k, n):
        v = ctr[k] % n
        ctr[k] += 1
        return v

    SQ128 = math.sqrt(128.0)
    RS = NKV * VW

    cast_ev = {}

    def do_cast1(src, dst, rows, name, i, npc=8):
        step = rows // npc
        cast_ev[name] = P.dma("pool", dst[i * step:(i + 1) * step, :], src[i * step:(i + 1) * step, :], skey=("cast", name))

    def do_cast_cols(i, npc, c0, c1, name):
        step = D // npc
        cast_ev[name] = P.dma("pool", Win_s[i * step:(i + 1) * step, c0:c1], w_in[i * step:(i + 1) * step, c0:c1], skey=("cast", name))

    later_q = [(i, 16, 0, 1024, "Win_q") for i in range(16)] + [(i, 16, 1536, 4608, "Win_q") for i in range(16)]
    later_casts = [(w_out, Wout_s, D, "Wout_s", i, 16) for i in range(16)] + [(w1, W1_s, D, "W1_s", i, 64) for i in range(64)] + \
                  [(w2, W2_s, DFF, "W2_s", i, 64) for i in range(64)]

    def p0():
        for i in range(8):
            do_cast_cols(i, 8, 4608, PROJ, "Win_kv")
        for i in range(2):
            do_cast_cols(i, 2, 1024, 1536, "Win_kv")
        P.dma("pool", ident[:, :], ident_in[:, :], writes=["ident"], skey="c0")
        P.dma("sp", validt[:, :], valid_in[:, :], writes=["validt"], skey="c1")
        P.op("dve", lambda e: e.memset(ones_bf[:, :], 1.0), writes=["ones_bf"])
        P.op("dve", lambda e: e.memset(ones_col[:, :, :], 1.0), writes=["ones_col"])
        for col, src in ((0, qna[0:1, :]), (1, kna[0:1, :]), (2, qnb[0:1, :]), (3, qnb[1:2, :]), (4, qnb[2:3, :]),
                         (5, knb[0:1, :]), (6, knb[1:2, :]), (7, knb[2:3, :])):
            P.dma("sp", gains[:, col:col + 1], bass.AP(src.tensor, src.offset, [[1, 128], [1, 1]]),
                  writes=["gains"], skey="c1")
        P.op("dve", lambda e: e.tensor_scalar(gains[:, 0:1], gains[:, 0:1], 1.0 / SQ128, None, ALU.mult), reads=["gains"], writes=["gains"])
        P.op("dve", lambda e: e.tensor_scalar(gains[:, 2:5], gains[:, 2:5], 1.0 / SQ128, None, ALU.mult), reads=["gains"], writes=["gains"])
        P.op("dve", lambda e: e.memset(epsc[:, :], EPS), writes=["epsc"])
        for i, src in enumerate((n1g_in, n2g_in)):
            P.dma("sp", n12g[:, i, :], bass.AP(src.tensor, src.offset, [[1, 128], [128, NKC]]),
                  writes=["n12g"], skey="c1", allow_slow_non_contiguous=True)
        P.dma("sp", es8[0:1, 0:8], sink_in[0:1, :], writes=["es8"], skey="c1")
        P.op("act", lambda e: e.activation(es8[0:1, 8:16], es8[0:1, 0:8], AF.Exp), reads=["es8"], writes=["es8b"])
        esb = es8[0:1, 8:16]
        P.op("dve", lambda e: e.tensor_copy(esrow[0:1, :, :], bass.AP(esb.tensor, esb.offset, [list(esb.ap[0]), [1, 8], [0, 128]])),
             reads=["es8b"], writes=["esrow"])
        xf = xres[:, :, :].rearrange("p a b -> p (a b)")
        Rext = xres[0:33, 3, 1024:1056]
        indt = xres[0:33, 2, 0:J]
        tvrow = xres[0:8, 3, 0:J]
        P.dma("sp", xres[0:32, 3, 1024:1056], relb[:, :], writes=[("x", 3)], skey="c3")
        P.op("dve", lambda e: e.memset(xres[32:33, 3, 1024:1056], 1.0), writes=[("x", 3)], reads=[("x", 3)])
        def bias_cfg(cfg):
            P.dma("sp", indt, ind_in[cfg], writes=[("x", 2)], skey="c3")

            def mmb(e, cfg=cfg):
                e.matmul(psA[0][0:8, 0:320], Rext[:, 8 * cfg:8 * cfg + 8], indt[:, 0:320], start=True, stop=True)
                return e.matmul(psA[1][0:8, 0:320], Rext[:, 8 * cfg:8 * cfg + 8], indt[:, 320:640], start=True, stop=True)
            P.op("pe", mmb, reads=[("x", 3), ("x", 2)], writes=[("psA", 0), ("psA", 1)])
            P.op("dve", lambda e: e.tensor_copy(tvrow[:, 0:320], psA[0][0:8, 0:320]), reads=[("psA", 0)], writes=[("x", 3)])
            P.op("dve", lambda e: e.tensor_copy(tvrow[:, 320:640], psA[1][0:8, 0:320]), reads=[("psA", 1)], writes=[("x", 3)])
            P.dma("sp", tv_s[8 * cfg:8 * cfg + 8, :], tvrow, reads=[("x", 3)], writes=["tv_s"], skey="c4")
            cmin = -128 if cfg == 0 else -64
            wdt = 384 if cfg == 0 else 256
            base = JH + cmin - 127
            for hh in range(8):
                P.dma("sp", xf[:, hh * 384:hh * 384 + wdt],
                      bass.AP(tv_s.tensor, tv_s.offset + (8 * cfg + hh) * J + base, [[1, 128], [1, wdt]]),
                      reads=["tv_s"], writes=[("x", 0), ("x", 1)], skey="c5")
            for hh in range(8):
                nt = 3 if cfg == 0 else 2
                for t_ in range(nt):
                    hk = xf[:, hh * 384 + t_ * 128 + 127:hh * 384 + t_ * 128 + 128]
                    rev = bass.AP(hk.tensor, hk.offset, [list(hk.ap[0]), [-1, 128]])
                    if cfg == 0:
                        dst = biasA[:, hh // 4, t_, hh % 4, :]
                    else:
                        dst = biasB[:, cfg - 1, hh, t_, :]
                    P.op("act", lambda e, dst=dst, rev=rev: e.activation(dst, rev, AF.Copy), reads=[("x", 0), ("x", 1)], writes=["bias"])
        for s_ in range(2):
            P.dma("sp", cT[:, :, s_], bass.AP(c2.tensor, c2.offset + s_ * D, [[1, 128], [128, NKC]]),
                  writes=["cT"], skey="c1", allow_slow_non_contiguous=True)
        P.op("act", lambda e: e.activation(cTb[:, :, :], cT[:, :, :], AF.Silu), reads=["cT"], writes=["cTb"])
        for ch in range(24):
            if ch in (3, 9, 15, 21):
                bias_cfg((ch - 3) // 6)
            c0 = ch * 512
            s = nxt("w", 2)
            P.dma("pool", Wr[s][:, :, :], w_ada[:, c0:c0 + 512].rearrange("(kc p) c -> p kc c", p=128),
                  writes=[("W", s)], skey=("W", s))
            m = ch % 2
            bsrc = b_ada[0:1, c0:c0 + 512]
            P.dma("sp", bada[m][0:2, :], bass.AP(bsrc.tensor, bsrc.offset, [[0, 2], [1, 512]]),
                  writes=[("rsf", m)], skey=("bada", m))
            a = nxt("a", 2)

            def mm(e, s=s, a=a):
                for kc in range(NKC):
                    ins = e.matmul(psA[a][0:2, :], cTb[:, kc, :], Wr[s][:, kc, :], start=(kc == 0), stop=(kc == NKC - 1))
                return ins
            P.op("pe", mm, reads=[("W", s), "cTb"], writes=[("psA", a)])
            P.op("dve", lambda e, a=a, m=m: e.tensor_tensor(modrow[m][0:2, :], psA[a][0:2, :], bada[m][0:2, :], ALU.add),
                 reads=[("psA", a), ("rsf", m)], writes=[("tmpf", m)])
            P.dma("sp", mod_s[:, c0:c0 + 512], modrow[m][0:2, :], reads=[("tmpf", m)], writes=["mod_s"], skey=("mods", m))
        for seg in range(2):
            for i in range(6):
                P.dma("sp", modP[:, seg, i, :],
                      bass.AP(mod_s.tensor, mod_s.offset + seg * 6 * D + i * D, [[1, 128], [128, NKC]]),
                      reads=["mod_s"], writes=["modP"], skey="c2", allow_slow_non_contiguous=True)
        for seg in range(2):
            P.op("dve", lambda e, seg=seg: e.scalar_tensor_tensor(scsh[:, seg, 0, :], modP[:, seg, 1, :], 1.0, n12g[:, 0, :], ALU.add, ALU.mult),
                 reads=["modP", "n12g"], writes=["scsh"])
            P.op("dve", lambda e, seg=seg: e.tensor_copy(scsh[:, seg, 1, :], modP[:, seg, 0, :]), reads=["modP"], writes=["scsh"])
            P.op("dve", lambda e, seg=seg: e.scalar_tensor_tensor(scsh[:, seg, 2, :], modP[:, seg, 4, :], 1.0, n12g[:, 1, :], ALU.add, ALU.mult),
                 reads=["modP", "n12g"], writes=["scsh"])
            P.op("dve", lambda e, seg=seg: e.tensor_copy(scsh[:, seg, 3, :], modP[:, seg, 3, :]), reads=["modP"], writes=["scsh"])
    def wload(scr, r0, c0, width, name):
        s = nxt("w", 2)
        P.dma("sp", Wr[s][:, :, 0:width], scr[r0:r0 + D, c0:c0 + width].rearrange("(kc p) c -> p kc c", p=128),
              writes=[("W", s)], skey=("W", s), extra=[cast_ev[name]])
        return s

    HB = [dict(t=hT, i=0), dict(t=QT[:, 0:16, :], i=1)]

    def hkey(hb, tt):
        return ("hT", hb["i"], tt)

    def norm_a(seg, which, tt, hb, src=None, skey=None):
        if src is None:
            src, skey = xres[:, tt, :], ("x", tt)
        skeys = skey if isinstance(skey, list) else [skey]
        h = hb["t"]
        P.op("dve", lambda e: e.memset(stat[:, 0:1], 0.0), writes=["stat"])
        P.op("act", lambda e: e.activation(h[:, :, tt * 128:(tt + 1) * 128], src.rearrange("p (a b) -> p a b", a=NKC), AF.Square, accum_out=stat[:, 0:1]),
             reads=skeys + ["stat"], writes=[hkey(hb, tt), "stat"])
        P.op("act", lambda e: e.activation(stat[:, 1:2], stat[:, 0:1], AF.Ln, bias=epsc[:, 0:1], scale=1.0 / D), reads=["stat", "epsc"], writes=["stat"])
        P.op("act", lambda e: e.activation(stat[:, 2:3], stat[:, 1:2], AF.Exp, scale=-0.5), reads=["stat"], writes=["stat"])
        P.op("dve", lambda e: e.tensor_scalar(xs[:, :], src, stat[:, 2:3], None, ALU.mult),
             reads=skeys + ["stat"], writes=["xs"])

    def norm_b(seg, which, tt, hb):
        h = hb["t"]
        for half in range(2):
            def tr(e, half=half):
                for j in range(8):
                    kc = half * 8 + j
                    ins = e.transpose(psT[half][:, j * 128:(j + 1) * 128], xs[:, kc * 128:(kc + 1) * 128], ident[:, :])
                return ins
            P.op("pe", tr, reads=["xs", "ident"], writes=[("psT", half)])

            def ev(js, on_act, half=half):
                def f(e):
                    for j in js:
                        kc = half * 8 + j
                        o = h[:, kc, tt * 128:(tt + 1) * 128]
                        i_ = psT[half][:, j * 128:(j + 1) * 128]
                        sc = scsh[:, seg, 2 * which, kc:kc + 1]
                        sh = scsh[:, seg, 2 * which + 1, kc:kc + 1]
                        if on_act:
                            ins = e.activation(o, i_, AF.Identity, bias=sh, scale=sc)
                        else:
                            ins = e.tensor_scalar(o, i_, sc, sh, ALU.mult, ALU.add)
                    return ins
                return f
            P.op("act", ev([0, 1, 2], True), reads=[("psT", half), "scsh"], writes=[hkey(hb, tt)])
            P.op("dve", ev([3, 4, 5, 6, 7], False), reads=[("psT", half), "scsh"], writes=[hkey(hb, tt)])

    def norm_tile(seg, which, tt, hb=None):
        hb = hb or HB[0]
        norm_a(seg, which, tt, hb)
        norm_b(seg, which, tt, hb)

    def norm_steps(seg, which, hb, srcs=None, pre=None):
        def mk(t_a, t_b):
            def f():
                if t_b is not None:
                    norm_b(seg, which, t_b, hb)
                if t_a is not None:
                    if pre:
                        pre(t_a)
                    if srcs:
                        norm_a(seg, which, t_a, hb, *srcs(t_a))
                    else:
                        norm_a(seg, which, t_a, hb)
            return f
        return [mk(0, None), mk(1, 0), mk(2, 1), mk(3, 2), mk(None, 3)]

    def ht_all(hb):
        return [hkey(hb, t_) for t_ in range(4)]
    HT_ALL = ht_all(HB[0])
    pend = []

    def flush_pend():
        while pend:
            pend.pop(0)()

    def qk_chunk(s, j, gcol, dest, dkey, after=None, hb=None):
        hb = hb or HB[0]
        hTt = hb["t"]
        a = nxt("a", 2)
        b = nxt("i", 2)

        def mmr(k0, k1):
            def mm(e):
                for kc in range(k0, k1):
                    ins = e.matmul(psA[a][:, :], Wr[s][:, kc, j * 128:(j + 1) * 128], hTt[:, kc, :], start=(kc == 0), stop=(kc == NKC - 1))
                return ins
            return mm
        P.op("pe", mmr(0, 5), reads=[("W", s)] + ht_all(hb), writes=[("psA", a)])
        flush_pend()
        P.op("pe", mmr(5, NKC), reads=[("W", s)] + ht_all(hb), writes=[("psA", a)])

        def tail():
            P.op("act", lambda e: e.activation(sqb[b][:, :], psA[a][:, :], AF.Square), reads=[("psA", a)], writes=[("sqb", b)])
            P.op("pe", lambda e: e.matmul(psS[:, :], ones_bf[:, :], sqb[b][:, :], start=True, stop=True),
                 reads=[("sqb", b), "ones_bf"], writes=["psS"])
            P.op("act", lambda e: e.activation(rsf[b][:, :], psS[:, :], AF.Ln, bias=epsc[:, 0:1], scale=1.0 / 128.0),
                 reads=["psS", "epsc"], writes=[("rsf", b)])
            P.op("act", lambda e: e.activation(rsf[b][:, :], rsf[b][:, :], AF.Exp, scale=-0.5), reads=[("rsf", b)], writes=[("rsf", b)])
            P.op("dve", lambda e: e.scalar_tensor_tensor(dest, psA[a][:, :], gains[:, gcol:gcol + 1], rsf[b][:, :], ALU.mult, ALU.mult),
                 reads=[("psA", a), ("rsf", b), "gains"], writes=[dkey])
            if after:
                after()
        pend.append(tail)

    store_evs = {}

    def p1_xload(seg, eb):
        segbase = 0 if seg == 0 else EP
        E0 = segbase + T * eb
        for tt in range(4):
            P.dma("sp", xres[:, tt, :], xe[E0 + tt * 128:E0 + (tt + 1) * 128, :], writes=[("x", tt)], skey=("x", tt))

    def p1_block(seg, eb, nb, hb, nxtblk, hbn):
        segbase = 0 if seg == 0 else EP
        E0 = segbase + T * eb
        full = 1 <= eb <= nb - 2
        hTt = hb["t"]
        kch = []
        vch = []
        if full:
            kch.append((1024, 256, [0, 1], 1))
            vch.append((1280, 256, 0, 2))
        for g in range(3):
            if not full and g != 2:
                continue
            for j in range(2):
                kch.append((4608 + g * 1024 + j * 512, 512, [2 + g * 8 + j * 4 + q for q in range(4)], 5 + g))
                vch.append((7680 + g * 1024 + j * 512, 512, 2 + g * 8 + j * 4, 4))
        for (c0, width, heads, gcol) in kch:
            s = wload(Win_s, 0, c0, width, "Win_kv")
            if later_q:
                do_cast_cols(*later_q.pop(0))
            elif later_casts:
                do_cast1(*later_casts.pop(0))
            for j, kvh in enumerate(heads):
                k = nxt("k", 2)

                def after(k=k, kvh=kvh):
                    store_evs[("k", k)] = P.dma("pool", KT_s[kvh, :, E0:E0 + T], kst[k][:, :], reads=[("kst", k)], skey=("kst", k))
                qk_chunk(s, j, gcol, kst[k][:, :], ("kst", k), after, hb=hb)
        flush_pend()
        steps = []
        xl = []
        if nxtblk is not None:
            nE0 = (0 if nxtblk[0] == 0 else EP) + T * nxtblk[1]
            xl = [lambda tt=tt: P.dma("sp", xres[:, tt, :], xe[nE0 + tt * 128:nE0 + (tt + 1) * 128, :], writes=[("x", tt)], skey=("x", tt))
                  for tt in range(4)]
            xl_all = list(xl)

            def need_x(t_):
                while len(xl) > 3 - t_:
                    xl.pop(0)()
            steps = norm_steps(nxtblk[0], 0, hbn, pre=need_x)
        ng = len(vch) * 4
        pos = {(i + 1) * ng // 6: i for i in range(5)} if steps else {}
        gi = 0
        for (c0, width, hv0, nh) in vch:
            s = wload(Win_s, 0, c0, width, "Win_kv")
            if later_q:
                do_cast_cols(*later_q.pop(0))
            elif later_casts:
                do_cast1(*later_casts.pop(0))
            for _ in range(2):
                if xl:
                    xl.pop(0)()
            for tt in range(4):
                a = nxt("a", 2)
                v = nxt("v", 2)
                tile = (E0 + tt * 128) // 128

                def mm(e, s=s, a=a, tt=tt, width=width):
                    for kc in range(NKC):
                        ins = e.matmul(psA[a][:, 0:width], hTt[:, kc, tt * 128:(tt + 1) * 128], Wr[s][:, kc, 0:width],
                                       start=(kc == 0), stop=(kc == NKC - 1))
                    return ins
                P.op("pe", mm, reads=[("W", s), hkey(hb, tt)], writes=[("psA", a)])
                P.op("act", lambda e, a=a, v=v, nh=nh, width=width, tile=tile: e.activation(
                    vst[v][:, 0:nh, 0:128], psA[a][:, 0:width].rearrange("p (h d) -> p h d", h=nh), AF.Copy,
                    scale=validt[:, tile:tile + 1]), reads=[("psA", a), "validt"], writes=[("vst", v)])
                P.op("dve", lambda e, v=v, nh=nh, tile=tile: e.tensor_scalar(
                    vst[v][:, 0:nh, 128:130], ones_col[:, 0:nh, :], validt[:, tile:tile + 1], None, ALU.mult),
                    reads=["ones_col", "validt", ("vst", v)], writes=[("vst", v)])
                store_evs[("v", v)] = P.dma("pool", V_s[E0 + tt * 128:E0 + (tt + 1) * 128, hv0:hv0 + nh, :], vst[v][:, 0:nh, :],
                                            reads=[("vst", v)], skey=("vst", v))
                gi += 1
                if gi in pos:
                    steps[pos[gi]]()
                    pos.pop(gi)
        for gi_ in sorted(pos):
            steps[pos[gi_]]()

    afifo = []
    ADEPTH = 4

    def a_push(fn, tags):
        afifo.append([fn, True, tags])
        while sum(1 for x in afifo if x[1]) >= ADEPTH:
            a_pop()

    def a_after(fn):
        if not afifo:
            fn()
        else:
            afifo.append([fn, False, ()])

    def a_pop():
        fn, counted, tags = afifo.pop(0)
        fn()
        while afifo and not afifo[0][1]:
            afifo.pop(0)[0]()

    def a_flush(tag=None):
        while afifo and (tag is None or any(tag in x[2] for x in afifo)):
            a_pop()

    def attn_units(units, num, den, nkey, dkey, depth=3):
        def qk(u, l):
            P.op("pe", lambda e: e.matmul(u.get("lgm", u["lg"])(psL[l]), u["lhsT"], u["rhs"], start=True, stop=True),
                 reads=u["rk"], writes=[PSLK[l]])
            P.op("dve", lambda e: e.tensor_tensor(u["lg"](psL[l]), u["lg"](psL[l]), u["bias"], ALU.add),
                 reads=[PSLK[l], "bias"], writes=[PSLK[l]])
            P.op("act", lambda e: e.activation(u["tf"](tmpb[l]), u["lg"](psL[l]), AF.Exp),
                 reads=[PSLK[l]], writes=[("tmpb", l)])

        def pv(u, l):
            def f(e):
                e.matmul(u["on"](num), u["v"], u["tf"](tmpb[l]), start=u["start"], stop=u["stop"], skip_group_check=True)
                return e.matmul(u["on"](den), u["vone"], u["tf"](tmpb[l]), start=u["start"], stop=u["stop"], skip_group_check=True)
            P.op("pe", f, reads=[("tmpb", l)] + u["vk"], writes=[nkey, dkey])
        for u in units:
            l = nxt("l", 4)
            qk(u, l)
            a_push(lambda u=u, l=l: pv(u, l), tuple(u["vk"]) + tuple(k for k in u["rk"] if isinstance(k, str)))

    def attn_super(units, num, den, nkey, dkey, depth=3):
        def qk(u, l):
            nk, nqp, npart = u["nk"], u["nqp"], len(u["parts"])
            tot = nqp * npart

            def f(e):
                for p_, pt in enumerate(u["parts"]):
                    ins = e.matmul(psL[l][0:nk, p_ * nqp:(p_ + 1) * nqp], pt["lhsT"], pt["rhs"], start=True, stop=True, skip_group_check=True)
                return ins
            P.op("pe", f, reads=u["rk"], writes=[PSLK[l]])
            P.op("dve", lambda e: e.tensor_tensor(psL[l][0:nk, 0:tot].rearrange("p (a b) -> p a b", a=npart),
                                                  psL[l][0:nk, 0:tot].rearrange("p (a b) -> p a b", a=npart), u["bias3"], ALU.add),
                 reads=[PSLK[l], "bias"], writes=[PSLK[l]])
            P.op("act", lambda e: e.activation(tmpb[l][0:nk, 0:tot], psL[l][0:nk, 0:tot], AF.Exp),
                 reads=[PSLK[l]], writes=[("tmpb", l)])

        def pv(u, l):
            nk, nqp = u["nk"], u["nqp"]

            def f(e):
                for p_, pt in enumerate(u["parts"]):
                    st = bool(u["first"] and p_ == 0)
                    e.matmul(pt["on"](num), pt["v"], tmpb[l][0:nk, p_ * nqp:(p_ + 1) * nqp], start=st, stop=False, skip_group_check=True)
                    ins = e.matmul(pt["on"](den), pt["vone"], tmpb[l][0:nk, p_ * nqp:(p_ + 1) * nqp], start=st, stop=False, skip_group_check=True)
                return ins
            P.op("pe", f, reads=[("tmpb", l)] + u["vk"], writes=[nkey, dkey])
        for u in units:
            l = nxt("l", 4)
            qk(u, l)
            a_push(lambda u=u, l=l: pv(u, l), tuple(u["vk"]) + tuple(k for k in u["rk"] if isinstance(k, str)))

    def bc128(ap_col, nk):
        return bass.AP(ap_col.tensor, ap_col.offset, [[ap_col.ap[0][0], nk], [0, 128]])

    def strided(t, col0, step, n, npart=128):
        base = t[0:npart, col0:col0 + 1]
        return bass.AP(base.tensor, base.offset, [[base.ap[0][0], npart], [step, n]])

    ND = [(psN, psS, "psN", "psS"), (psA[0], psA[1], ("psA", 0), ("psA", 1))]
    ndc = [0]

    YTf = YT.bitcast(F32)[:, :, :].rearrange("p a b -> p (a b)")
    YT_KEYS = [("YT", i) for i in range(10)]
    ystores = []

    def p2_block(seg, b, prenormed, nxtblk):
        segbase = 0 if seg == 0 else EP
        E0 = segbase + HALO + T * b
        orow = (0 if seg == 0 else LP) + T * b
        if not prenormed:
            for tt in range(4):
                P.dma("sp", xres[:, tt, :], xe[E0 + tt * 128:E0 + (tt + 1) * 128, :], writes=[("x", tt)], skey=("x", tt))
            for tt in range(4):
                norm_tile(seg, 0, tt)
        qch = [(0, list(range(0, 4)), 0), (512, list(range(4, 8)), 0)]
        for g in range(3):
            for j in range(2):
                qch.append((1536 + g * 1024 + j * 512, [8 + g * 8 + j * 4 + q for q in range(4)], 2 + g))
        for qi, (c0, heads, gcol) in enumerate(qch):
            s = wload(Win_s, 0, c0, 512, "Win_q")
            if qi >= 2 and ystores:
                ystores.pop(0)()
            for j, hq in enumerate(heads):
                qk_chunk(s, j, gcol, QT[:, hq, :], ("QT", hq))
        flush_pend()
        while ystores:
            ystores.pop(0)()
        xrl = []
        if prenormed:
            xrl = [lambda tt=tt: P.dma("sp", xres[:, tt, :], xe[E0 + tt * 128:E0 + (tt + 1) * 128, :], writes=[("x", tt)], skey=("x", tt))
                   for tt in range(4)]
        kv_extra = [v for k, v in store_evs.items() if k[0] in ('k', 'v')]
        for kh in range(2):
            a_flush("VT0")
            a_flush("KW0")
            P.dma("sp", KW[0][:, 0:768], KT_s[kh, :, E0 - 128:E0 + 640], writes=["KW0"], skey="KW0", extra=kv_extra)
            P.dma("sp", VT0[:, 0:6, :], V_s[E0 - 128:E0 + 640, kh, :].rearrange("(j p) c -> p j c", p=128),
                  writes=["VT0"], skey="VT0", extra=kv_extra)
            for qb in range(4):
                num, den, nkey, dkey = ND[ndc[0] % 2]
                ndc[0] += 1
                units = []
                for rel in range(3):
                    j = qb + rel
                    units.append(dict(
                        lgm=lambda p_: p_[:, :].rearrange("p (h q) -> p h q", h=4),
                        lg=lambda p_: p_[:, :],
                        tf=lambda t_: t_[:, :],
                        lhsT=KW[0][:, j * 128:(j + 1) * 128],
                        rhs=QT[:, 4 * kh:4 * kh + 4, qb * 128:(qb + 1) * 128],
                        rk=["KW0"] + [("QT", 4 * kh + q) for q in range(4)],
                        bias=biasA[:, kh, rel, :, :].rearrange("p h q -> p (h q)"),
                        v=VT0[:, j, 0:128], vone=bc128(VT0[:, j, 128:129], 128), vk=["VT0"],
                        on=lambda n_: n_[:, :], start=(rel == 0), stop=False))
                attn_units_A(units, num, den, nkey, dkey)

                def fin_a(num=num, den=den, nkey=nkey, dkey=dkey, kh=kh, qb=qb):
                    P.op("pe", lambda e: e.matmul(den[:, :], ones_bf[0:1, 0:128],
                                                  esrow[0:1, 4 * kh:4 * kh + 4, :].rearrange("p h q -> p (h q)"),
                                                  start=False, stop=True, skip_group_check=True),
                         reads=["esrow", "ones_bf"], writes=[dkey])
                    P.op("act", lambda e: e.activation(denf[:, :], den[:, :], AF.Ln), reads=[dkey], writes=["denf"])
                    P.op("act", lambda e: e.activation(denf[:, :], denf[:, :], AF.Exp, scale=-1.0), reads=["denf"], writes=["denf"])
                    P.op("dve", lambda e: e.tensor_tensor(
                        YT[:, 4 * kh:4 * kh + 4, qb * 128:(qb + 1) * 128], num[:, :].rearrange("p (h q) -> p h q", h=4),
                        denf[:, :].rearrange("p (h q) -> p h q", h=4), ALU.mult),
                        reads=[nkey, "denf"], writes=[("YT", kh)])
                a_after(fin_a)
        for h in range(8):
            num, den, nkey, dkey = ND[ndc[0] % 2]
            ndc[0] += 1
            if xrl:
                xrl.pop(0)()
            for g in range(3):
                r = CFG[g + 1][0]
                hv = 2 + 8 * g + h
                hq = 8 + 8 * g + h
                wd = T + 128 * r
                kwk = "KW%d" % g
                vtk = "VT%d" % g
                if g == 2:
                    for q4 in range(4):
                        a_flush(("VT2", q4))
                else:
                    a_flush(vtk)
                a_flush(kwk)
                P.dma("sp", KW[g][:, 0:wd], KT_s[hv, :, E0 - 64 * r:E0 - 64 * r + wd], writes=[kwk], skey=kwk, extra=kv_extra)
                vbase = V_s[E0 - 64 * r:E0 - 64 * r + 1, hv, :]
                if g == 0:
                    P.dma("sp", VT0[:, 0:5, :], V_s[E0 - 64:E0 + 576, hv, :].rearrange("(j p) c -> p j c", p=128),
                          writes=[vtk], skey=vtk, extra=kv_extra)
                elif g == 1:
                    P.dma("sp", VT1[:, :, :, :],
                          bass.AP(vbase.tensor, vbase.offset, [[4 * RS, 128], [RS, 4], [512 * RS, 2], [1, VW]]),
                          writes=[vtk], skey=vtk, extra=kv_extra)
                else:
                    for q4 in range(4):
                        P.dma("sp", VT2[:, 4 * q4:4 * q4 + 4, 0, :],
                              bass.AP(vbase.tensor, vbase.offset + 4 * q4 * RS, [[16 * RS, 128], [RS, 4], [1, VW]]),
                              writes=[("VT2", q4)], skey=("VT2", q4), extra=kv_extra)
                        P.dma("sp", VT2[0:32, 4 * q4:4 * q4 + 4, 1, :],
                              bass.AP(vbase.tensor, vbase.offset + (16 * 128 + 4 * q4) * RS, [[16 * RS, 32], [RS, 4], [1, VW]]),
                              writes=[("VT2", q4)], skey=("VT2", q4), extra=kv_extra)
                units = []
                nqp = 128 if g < 2 else 32
                npart = 4
                combos = [(0, kt) for kt in range(2)] if g < 2 else [(q4, kt) for q4 in range(4) for kt in range(2)]
                for (q4_, kt) in combos:
                    nk = 32 if (g == 2 and kt == 1) else 128
                    parts = []
                    vkk = [("VT2", q4_)] if g == 2 else [vtk]
                    for c_ in range(4 * q4_, 4 * q4_ + 4):
                        if g == 0:
                            j = c_ + kt
                            lhsT = KW[0][:, j * 128:(j + 1) * 128]
                            vt = VT0[:, j, :]
                            qcol = lambda t_, c_=c_: t_[:, c_ * 128:(c_ + 1) * 128]
                            rhs = QT[:, hq, c_ * 128:(c_ + 1) * 128]
                        else:
                            lhsT = strided(KW[g], c_ + r * 128 * kt, r, nk)
                            vt = (VT1 if g == 1 else VT2)[0:nk, c_, kt, :]
                            qcol = lambda t_, c_=c_, r=r, nqp=nqp: strided(t_, c_, r, nqp)
                            rhs = strided_q(QT, hq, c_, r, nqp)
                        parts.append(dict(lhsT=lhsT, rhs=rhs, v=vt[:, 0:128], vone=bc128(vt[:, 128:129], nk), on=qcol))
                    bsl = biasB[0:nk, g, h, kt, 0:nqp]
                    units.append(dict(parts=parts, nk=nk, nqp=nqp, rk=[kwk, ("QT", hq)], vk=vkk,
                                      bias3=bass.AP(bsl.tensor, bsl.offset, [list(bsl.ap[0]), [0, npart], [1, nqp]]),
                                      first=(g == 0 and kt == 0 and q4_ == 0)))
                attn_super(units, num, den, nkey, dkey)
            def fin_b(num=num, den=den, nkey=nkey, dkey=dkey, h=h):
                P.op("act", lambda e: e.activation(denf[:, :], den[:, :], AF.Ln), reads=[dkey], writes=["denf"])
                P.op("act", lambda e: e.activation(denf[:, :], denf[:, :], AF.Exp, scale=-1.0), reads=["denf"], writes=["denf"])
                P.op("dve", lambda e: e.tensor_tensor(YT[:, 8 + h, :], num[:, :], denf[:, :], ALU.mult),
                     reads=[nkey, "denf"], writes=[("YT", 2 + h)])
            a_after(fin_b)
        a_flush()
        YT_ALL = [("YT", i) for i in range(10)]
        if dbg and seg == 0 and b == 0:
            P.dma("sp", dbgQ[:, :, :], QT[:, :, :], reads=[("QT", i) for i in range(32)], skey="dbg")
            P.dma("sp", dbgY[:, :, :], YT[:, :, :], reads=YT_ALL, skey="dbg")
        for pair in range(2):
            for cc in range(4):
                s = wload(Wout_s, 0, cc * 512, 512, "Wout_s")
                for tt in (2 * pair, 2 * pair + 1):
                    a = nxt("a", 2)
                    i_ = nxt("i", 2)

                    def mm(e, s=s, a=a, tt=tt):
                        for kc in range(NKC):
                            ins = e.matmul(psA[a][:, :], YT[:, kc, tt * 128:(tt + 1) * 128], Wr[s][:, kc, :], start=(kc == 0), stop=(kc == NKC - 1))
                        return ins
                    P.op("pe", mm, reads=[("W", s)] + YT_ALL, writes=[("psA", a)])
                    P.op("dve", lambda e, a=a, i_=i_, cc=cc: e.tensor_tensor(tmpf[i_][:, :], psA[a][:, :], gbc[:, 0, cc * 512:(cc + 1) * 512], ALU.mult),
                         reads=[("psA", a), "gbc"], writes=[("tmpf", i_)])
                    P.op("pool", lambda e, i_=i_, tt=tt, cc=cc: e.tensor_tensor(xres[:, tt, cc * 512:(cc + 1) * 512], tmpf[i_][:, :],
                                                                              xres[:, tt, cc * 512:(cc + 1) * 512], ALU.add),
                         reads=[("tmpf", i_), ("x", tt)], writes=[("x", tt)])
                if pair == 1:
                    if cc == 0:
                        norm_a(seg, 1, 0, HB[0])
                    elif cc == 1:
                        norm_b(seg, 1, 0, HB[0])
                        norm_a(seg, 1, 1, HB[0])
                    elif cc == 2:
                        norm_b(seg, 1, 1, HB[0])
        for tt in (2, 3):
            norm_a(seg, 1, tt, HB[0])
            norm_b(seg, 1, tt, HB[0])
        FT_ALL = [("QT", i) for i in range(32)]
        accs = [(psA[0], ("psA", 0)), (psA[1], ("psA", 1)), (psS, "psS"), (psN, "psN")]
        for half in range(2):
            for c8 in range(8):
                s = wload(W1_s, 0, half * 4096 + c8 * 512, 512, "W1_s")
                for j in range(4):
                    ffl = c8 * 4 + j
                    a = nxt("a", 2)
                    i_ = nxt("i", 2)

                    def mm(e, s=s, a=a, j=j):
                        for kc in range(NKC):
                            ins = e.matmul(psA[a][:, :], Wr[s][:, kc, j * 128:(j + 1) * 128], hT[:, kc, :], start=(kc == 0), stop=(kc == NKC - 1))
                        return ins
                    P.op("pe", mm, reads=[("W", s)] + HT_ALL, writes=[("psA", a)])
                    P.op("act", lambda e, a=a, i_=i_: e.activation(tmpf[i_][:, :], psA[a][:, :], AF.Relu),
                         reads=[("psA", a)], writes=[("tmpf", i_)])
                    P.op("pool", lambda e, i_=i_, ffl=ffl: e.tensor_tensor(QT[:, ffl, :], tmpf[i_][:, :], tmpf[i_][:, :], ALU.mult),
                         reads=[("tmpf", i_)], writes=[("QT", ffl)])
            pre = (half == 1 and nxtblk is not None)
            if pre:
                nseg, nb_ = nxtblk
                En = (0 if nseg == 0 else EP) + HALO + T * nb_

                def stage(t_):
                    sl = t_ % 2
                    P.dma("sp", YTf[:, sl * D:(sl + 1) * D], xe[En + t_ * 128:En + (t_ + 1) * 128, :],
                          writes=YT_KEYS + [("YTs", sl)], skey=("YTs", sl))

                def na(t_):
                    sl = t_ % 2
                    norm_a(nseg, 0, t_, HB[0], YTf[:, sl * D:(sl + 1) * D], [("YTs", sl)] + YT_KEYS)
                stage(0)
                stage(1)
                na(0)
            for cc in range(4):
                for q in range(2):
                    if pre and cc == 3 and q == 1:
                        norm_b(nseg, 0, 3, HB[0])
                    s = wload(W2_s, half * 4096 + q * D, cc * 512, 512, "W2_s")
                    for tt in range(4):
                        acc, akey = accs[tt]

                        def mm(e, s=s, acc=acc, tt=tt, q=q):
                            for kc in range(NKC):
                                ins = e.matmul(acc[:, :], QT[:, q * 16 + kc, tt * 128:(tt + 1) * 128], Wr[s][:, kc, :],
                                               start=(q == 0 and kc == 0), stop=(q == 1 and kc == NKC - 1), skip_group_check=True)
                            return ins
                        P.op("pe", mm, reads=[("W", s)] + FT_ALL, writes=[akey])
                for tt in range(4):
                    acc, akey = accs[tt]
                    i_ = nxt("i", 2)
                    P.op("dve", lambda e, acc=acc, i_=i_, cc=cc: e.tensor_tensor(tmpf[i_][:, :], acc[:, :], gbc[:, 1, cc * 512:(cc + 1) * 512], ALU.mult),
                         reads=[akey, "gbc"], writes=[("tmpf", i_)])
                    P.op("pool", lambda e, i_=i_, tt=tt, cc=cc: e.tensor_tensor(xres[:, tt, cc * 512:(cc + 1) * 512], tmpf[i_][:, :],
                                                                              xres[:, tt, cc * 512:(cc + 1) * 512], ALU.add),
                         reads=[("tmpf", i_), ("x", tt)], writes=[("x", tt)])
                if pre:
                    if cc == 0:
                        norm_b(nseg, 0, 0, HB[0])
                        na(1)
                        stage(2)
                    elif cc == 1:
                        norm_b(nseg, 0, 1, HB[0])
                        na(2)
                        stage(3)
                    elif cc == 2:
                        norm_b(nseg, 0, 2, HB[0])
                        na(3)
        def do_store(tt, orow=orow):
            store_evs[("y", tt)] = P.dma("sp", yo[orow + tt * 128:orow + (tt + 1) * 128, :], xres[:, tt, :], reads=[("x", tt)], skey=("y", tt))
        for tt in range(4):
            if nxtblk is None:
                do_store(tt)
            else:
                ystores.append(lambda tt=tt: do_store(tt))

    def strided_q(t, hq, col0, step, n):
        base = t[:, hq, col0:col0 + 1]
        return bass.AP(base.tensor, base.offset, [list(base.ap[0]), [step, n]])

    def attn_units_A(units, num, den, nkey, dkey):
        attn_units(units, num, den, nkey, dkey)

    def load_gbc(seg):
        for i, blk in enumerate((2, 5)):
            srcp = mod_s[seg:seg + 1, blk * D:(blk + 1) * D]
            P.dma("pool", gbc[:, i, :], bass.AP(srcp.tensor, srcp.offset, [[0, 128], [1, D]]), reads=["mod_s"], writes=["gbc"], skey="gbc")

    p0()
    NBP, NBS = EP // T, ES // T
    if "p1" in stages:
        BL = [(0, eb, NBP) for eb in range(NBP)] + [(1, eb, NBS) for eb in range(NBS)]
        p1_xload(BL[0][0], BL[0][1])
        for tt in range(4):
            norm_tile(BL[0][0], 0, tt, HB[0])
        for i, (sg, eb, nb_) in enumerate(BL):
            p1_block(sg, eb, nb_, HB[i % 2], BL[i + 1] if i + 1 < len(BL) else None, HB[(i + 1) % 2])
    while later_q:
        do_cast_cols(*later_q.pop(0))
    while later_casts:
        do_cast1(*later_casts.pop(0))
    if "p2" in stages:
        B2 = [(0, b) for b in range(LP // T)] + [(1, b) for b in range(LS // T)]
        for i, (sg, b) in enumerate(B2):
            if b == 0:
                load_gbc(sg)
            p2_block(sg, b, i > 0, B2[i + 1] if i + 1 < len(B2) else None)
    P.final_wait("sp", list(store_evs.values()) + [(P.esem[k], P.ecount[k], k) for k in ENG if P.ecount[k] > 0])

    with nc.Block() as block:
        @block.sync
        def _(e):
            for f in P.ops["sp"]:
                f(e)

        @block.scalar
        def _(e):
            for f in P.ops["act"]:
                f(e)

        @block.vector
        def _(e):
            for f in P.ops["dve"]:
                f(e)

        @block.gpsimd
        def _(e):
            for f in P.ops["pool"]:
                f(e)

        @block.tensor
        def _(e):
            for f in P.ops["pe"]:
                f(e)
    return nc


def dbg_emit(L):
    pass


def rel_bucket_np(rel):
    half, exact = 16, 8
    n = np.abs(rel)
    large = exact + (np.log(np.maximum(n, 1).astype(np.float32) / np.float32(exact)).astype(np.float32)
                     / np.float32(math.log(1024 / exact)) * np.float32(half - exact)).astype(np.int32)
    large = np.minimum(large, half - 1)
    return np.where(rel > 0, half, 0) + np.where(n < exact, n, large)


def make_ind():
    ind = np.zeros((4, 33, J), np.float32)
    u = np.arange(J)
    delta = u - JH
    for cfg, (r, rad) in enumerate(CFG):
        inb = np.abs(delta) <= rad
        bk = rel_bucket_np(delta * r)
        ind[cfg, bk[inb], u[inb]] = 1.0
        ind[cfg, 32, ~inb] = NEG
    return ind


_NC_CACHE = {}


def core_inputs(inputs, c):
    s, qt = c // 4, c % 4
    xe = np.zeros((NEXT, D), np.float32)
    valid = np.zeros((NEXT,), np.float32)
    for (arr, L, S, base) in ((inputs["x_prompt"][s], LP, 16384, 0), (inputs["x_sample"][s], LS, 4096, EP)):
        lo = qt * L - HALO
        hi = (qt + 1) * L + HALO
        a, b = max(lo, 0), min(hi, S)
        xe[base + (a - lo):base + (b - lo)] = arr[a:b]
        valid[base + (a - lo):base + (b - lo)] = 1.0
    c2 = np.stack([inputs["c_prompt"][s], inputs["c_sample"][s]]).astype(np.float32)
    return xe, np.ascontiguousarray(valid.reshape(NEXT // 128, 128).T), c2


def kernel(**inputs):
    inputs = {k: np.asarray(v) for k, v in inputs.items()}
    if "nc" not in _NC_CACHE:
        _NC_CACHE["nc"] = build()
    nc = _NC_CACHE["nc"]
    shared = {
        "ind": make_ind(), "ident": np.eye(128, dtype=np.float32),
        "w_in": inputs["w_in"][0], "w_out": inputs["w_out"][0], "w1": inputs["w1"][0], "w2": inputs["w2"][0],
        "w_ada": inputs["w_ada"][0], "b_ada": inputs["b_ada"], "norm1_g": inputs["norm1_g"], "norm2_g": inputs["norm2_g"],
        "q_norm_a": inputs["q_norm_a"], "k_norm_a": inputs["k_norm_a"], "sink_a": inputs["sink_a"],
        "q_norm_b": inputs["q_norm_b"][0], "k_norm_b": inputs["k_norm_b"][0], "rel_bias": inputs["rel_bias"],
    }
    shared = {k: np.ascontiguousarray(v, dtype=np.float32) for k, v in shared.items()}
    in_maps = []
    for c in range(8):
        xe, valid, c2 = core_inputs(inputs, c)
        m = dict(shared)
        m.update({"xe": xe, "valid": valid, "c2": c2})
        in_maps.append(m)
    res = run_bass_kernel_spmd(nc, in_maps, core_ids=list(range(8)))
    yp = np.zeros((2, 16384, D), np.float32)
    ysm = np.zeros((2, 4096, D), np.float32)
    for c in range(8):
        s, qt = c // 4, c % 4
        y = res.results[c]["yo"]
        yp[s, qt * LP:(qt + 1) * LP] = y[0:LP]
        ysm[s, qt * LS:(qt + 1) * LS] = y[LP:LP + LS]
    return (yp, ysm)
```
